# Optimizing a Trainium2 kernel written in Bass

```python
import math
import jax, jax.numpy as jnp
from jax import lax
import numpy as np


D_MODEL = 1024
BATCH = 8
SEQ = 4096
DEPTH = 1

CHUNK = 64
Q_BLOCK = 128
D_CONV = 512
CONV_K = 3
N_HEADS = 4
HEAD_DIM = 64
V_HEAD_DIM = 2 * HEAD_DIM
ATTN_QK = N_HEADS * 2 * HEAD_DIM
ATTN_V = N_HEADS * V_HEAD_DIM
D_FF = int(math.ceil(8 * D_MODEL / 3 / 256)) * 256
ROPE_THETA = 10000.0
EPS = 1e-6
SPLITS = (D_CONV, D_CONV, D_CONV, ATTN_QK, ATTN_QK, ATTN_V, D_MODEL, D_MODEL)
IN_COLS = sum(SPLITS)
SPLIT_IDX = [int(v) for v in np.cumsum(SPLITS)[:-1]]

kernel_name = "hybrid_shortconv_diffattn_block"


def _rmsnorm(x, g):
    xf = x.astype(jnp.float32)
    y = xf * lax.rsqrt(jnp.mean(xf * xf, axis=-1, keepdims=True) + EPS)
    return (y * g.astype(jnp.float32)).astype(x.dtype)


def _rope_tables(seq):
    pos = jnp.arange(seq, dtype=jnp.float32)
    inv = 1.0 / (ROPE_THETA ** (jnp.arange(0, HEAD_DIM, 2, dtype=jnp.float32) / HEAD_DIM))
    ang = pos[:, None] * inv[None, :]
    ang = jnp.concatenate([ang, ang], axis=-1)
    return jnp.cos(ang), jnp.sin(ang)


def _apply_rope(x, cos, sin):
    xf = x.astype(jnp.float32)
    x1, x2 = jnp.split(xf, 2, axis=-1)
    rot = jnp.concatenate([-x2, x1], axis=-1)
    c = cos[:, None, None, :]
    s = sin[:, None, None, :]
    return (xf * c + rot * s).astype(x.dtype)


def _short_conv_mixer(b_gate, c_gate, v, conv_w):
    u = c_gate * v
    y = lax.conv_general_dilated(
        u, conv_w[:, None, :].astype(u.dtype), window_strides=(1,),
        padding=[(CONV_K - 1, 0)], dimension_numbers=('NWC', 'WIO', 'NWC'),
        feature_group_count=D_CONV)
    return b_gate * y


def _diff_attention(q, k, v, q_norm, k_norm, lq1, lk1, lq2, lk2, sub_norm, lam_init, cos, sin):
    B, S = q.shape[0], q.shape[1]
    q = q.reshape(B, S, N_HEADS, 2, HEAD_DIM)
    k = k.reshape(B, S, N_HEADS, 2, HEAD_DIM)
    q = _apply_rope(_rmsnorm(q, q_norm), cos, sin)
    k = _apply_rope(_rmsnorm(k, k_norm), cos, sin)
    f32 = jnp.float32
    lam = (jnp.exp(jnp.sum(lq1.astype(f32) * lk1.astype(f32)))
           - jnp.exp(jnp.sum(lq2.astype(f32) * lk2.astype(f32))) + lam_init)
    q = q.transpose(3, 0, 2, 1, 4)
    k = k.transpose(3, 0, 2, 1, 4)
    vh = v.reshape(B, S, N_HEADS, V_HEAD_DIM).transpose(0, 2, 1, 3)
    nb = S // Q_BLOCK
    qb = q.reshape(2, B, N_HEADS, nb, Q_BLOCK, HEAD_DIM).transpose(3, 0, 1, 2, 4, 5)
    key_chunk = jnp.arange(S) // CHUNK
    scale = HEAD_DIM ** -0.5

    def block(args):
        qblk, i = args
        q_chunk = (i * Q_BLOCK + jnp.arange(Q_BLOCK)) // CHUNK
        mask = key_chunk[None, :] <= q_chunk[:, None]
        s = jnp.einsum('nbhqd,nbhkd->nbhqk', qblk, k).astype(f32) * scale
        p = jax.nn.softmax(jnp.where(mask, s, -jnp.inf), axis=-1)
        w = (p[0] - lam * p[1]).astype(vh.dtype)
        return jnp.einsum('bhqk,bhkd->bhqd', w, vh)

    o = lax.map(block, (qb, jnp.arange(nb)))
    o = o.transpose(1, 0, 3, 2, 4).reshape(B, S, N_HEADS, V_HEAD_DIM)
    o = _rmsnorm(o, sub_norm) * (1.0 - lam_init)
    return o.reshape(B, S, ATTN_V)


def setup_inputs(seed: int = 0) -> dict:
    key = jax.random.key(seed)
    ks = jax.random.split(key, 20)
    n = jax.random.normal
    f = jnp.float32
    L = DEPTH
    return {
        "x": n(ks[0], (BATCH, SEQ, D_MODEL), f),
        "g_mix": 1.0 + 0.02 * n(ks[1], (L, D_MODEL), f),
        "w_in": n(ks[2], (L, D_MODEL, IN_COLS), f) * D_MODEL ** -0.5,
        "b_gate": 0.02 * n(ks[3], (L, 2 * D_MODEL), f),
        "conv_w": n(ks[4], (L, CONV_K, D_CONV), f) * CONV_K ** -0.5,
        "q_norm": 1.0 + 0.02 * n(ks[5], (L, HEAD_DIM), f),
        "k_norm": 1.0 + 0.02 * n(ks[6], (L, HEAD_DIM), f),
        "lambda_q1": 0.1 * n(ks[7], (L, HEAD_DIM), f),
        "lambda_k1": 0.1 * n(ks[8], (L, HEAD_DIM), f),
        "lambda_q2": 0.1 * n(ks[9], (L, HEAD_DIM), f),
        "lambda_k2": 0.1 * n(ks[10], (L, HEAD_DIM), f),
        "sub_norm": 1.0 + 0.02 * n(ks[11], (L, V_HEAD_DIM), f),
        "w_conv_out": n(ks[12], (L, D_CONV, D_MODEL), f) * D_CONV ** -0.5,
        "w_attn_out": n(ks[13], (L, ATTN_V, D_MODEL), f) * ATTN_V ** -0.5,
        "w_o": n(ks[14], (L, D_MODEL, D_MODEL), f) * D_MODEL ** -0.5,
        "g_ffn": 1.0 + 0.02 * n(ks[15], (L, D_MODEL), f),
        "w_gate_up": n(ks[16], (L, D_MODEL, 2 * D_FF), f) * D_MODEL ** -0.5,
        "w_down": n(ks[17], (L, D_FF, D_MODEL), f) * D_FF ** -0.5,
    }


def reference(x, g_mix, w_in, b_gate, conv_w, q_norm, k_norm, lambda_q1, lambda_k1,
              lambda_q2, lambda_k2, sub_norm, w_conv_out, w_attn_out, w_o, g_ffn,
              w_gate_up, w_down):
    S = x.shape[1]
    cos, sin = _rope_tables(S)
    for l in range(DEPTH):
        lam_init = 0.8 - 0.6 * math.exp(-0.3 * l)
        h = _rmsnorm(x, g_mix[l])
        z = jnp.einsum('bsd,dc->bsc', h, w_in[l])
        bc, cc, vc, q, k, va, gc_pre, ga_pre = jnp.split(z, SPLIT_IDX, axis=-1)
        g_c = jax.nn.sigmoid(gc_pre + b_gate[l][:D_MODEL])
        g_a = jax.nn.sigmoid(ga_pre + b_gate[l][D_MODEL:])
        y_c = _short_conv_mixer(bc, cc, vc, conv_w[l]) @ w_conv_out[l]
        y_a = _diff_attention(q, k, va, q_norm[l], k_norm[l], lambda_q1[l], lambda_k1[l],
                              lambda_q2[l], lambda_k2[l], sub_norm[l], lam_init, cos, sin) @ w_attn_out[l]
        x = x + (g_c * y_c + g_a * y_a) @ w_o[l]
        h2 = _rmsnorm(x, g_ffn[l])
        gt, up = jnp.split(h2 @ w_gate_up[l], 2, axis=-1)
        x = x + (jax.nn.silu(gt) * up) @ w_down[l]
    return x
```

```python
import contextlib
import math

import numpy as np
import concourse.bass as bass
import concourse.mybir as mybir
from concourse.bass_utils import run_bass_kernel_spmd

F32 = mybir.dt.float32
BF16 = mybir.dt.bfloat16
I32 = mybir.dt.int32
AF = mybir.ActivationFunctionType
ALU = mybir.AluOpType
AX = mybir.AxisListType

D = 1024
S = 4096
NCORES = 8
T = 512
NST = S // T
DFF = 2816
NFC = DFF // 128
EPS = 1e-6
LAM_INIT = 0.8 - 0.6 * math.exp(0.0)
NBLK = 31
NSLOT = 3
LOOKAHEAD = 2
NDUP = 0

C_GMIX, C_GFFN, C_BG, C_CW, C_QN, C_KN, C_LAM, C_SN, C_INV, C_END = 0, 8, 16, 32, 44, 108, 172, 428, 429, 461
CW = 464


class Sched:
    def __init__(self, nc, es, needed=None, chans=()):
        self.nc = nc
        self.es = es
        self.emit_mode = needed is not None
        self.needed = needed
        self.ops = []
        self.deps_all = []
        self.lastw = {}
        self.readers = {}
        self.engs = {"pe": nc.tensor, "act": nc.scalar, "dve": nc.vector, "pool": nc.gpsimd, "sp": nc.sync}
        self.chan_val = {}
        if self.emit_mode:
            self.sems = {e: es.enter_context(nc.semaphore("sem_" + e)) for e in ("pe", "act", "dve", "pool")}
            self.csems = {c: es.enter_context(nc.semaphore("ch_" + "_".join(str(k) for k in (c if isinstance(c, tuple) else (c,)))))
                          for c in chans}
            self.counts = {e: 0 for e in self.sems}
            self.seen = {e: {} for e in self.engs}

    def _deps(self, reads, writes):
        deps = set()
        for r in reads:
            w = self.lastw.get(r)
            if w is not None:
                deps.add(w)
        for w_ in writes:
            w = self.lastw.get(w_)
            if w is not None:
                deps.add(w)
            for rd in self.readers.get(w_, {}).values():
                deps.add(rd)
        return deps

    def _commit(self, idx, skey, reads, writes):
        for r in reads:
            self.readers.setdefault(r, {})[skey] = idx
        for w in writes:
            self.lastw[w] = idx
            self.readers[w] = {}

    def _waits(self, eng_name, kind, deps):
        eng = self.engs[eng_name]
        sn = self.seen[eng_name]
        waits = {}
        for d in deps:
            p_eng, p_kind, p_chan, p_v = self.ops[d]
            if p_kind == "c":
                if p_eng == "pe" and eng_name == "pe" and kind == "c":
                    continue
                key = ("e", p_eng)
            else:
                key = ("c", p_chan)
            if p_v > waits.get(key, -1):
                waits[key] = p_v
        for key, v in waits.items():
            if sn.get(key, -1) >= v:
                continue
            sn[key] = v
            sem = self.sems[key[1]] if key[0] == "e" else self.csems[key[1]]
            eng.wait_ge(sem, v)

    def op(self, eng, fn, reads=(), writes=()):
        deps = self._deps(reads, writes)
        idx = len(self.ops)
        if self.emit_mode:
            self._waits(eng, "c", deps)
            ins = fn()
            cnt = None
            if self.needed[idx]:
                self.counts[eng] += 1
                cnt = self.counts[eng]
                ins.then_inc(self.sems[eng], 1)
            self.ops.append((eng, "c", None, cnt))
        else:
            self.ops.append((eng, "c", None, None))
            self.deps_all.append(deps)
        self._commit(idx, eng, reads, writes)
        return idx

    def dma(self, queue, fns, reads, writes, chan):
        deps = self._deps(reads, writes)
        idx = len(self.ops)
        self.chan_val[chan] = self.chan_val.get(chan, 0) + 16 * len(fns)
        if self.emit_mode:
            self._waits(queue, "d", deps)
            for f in fns:
                f().then_inc(self.csems[chan], 16)
        else:
            self.deps_all.append(deps)
        self.ops.append((queue, "d", chan, self.chan_val[chan]))
        self._commit(idx, ("ch", chan), reads, writes)
        return idx

    def analyze(self):
        needed = [False] * len(self.ops)
        for i, deps in enumerate(self.deps_all):
            o_eng, o_kind, _, _ = self.ops[i]
            for d in deps:
                p_eng, p_kind, _, _ = self.ops[d]
                if p_kind == "c" and not (p_eng == "pe" and o_eng == "pe" and o_kind == "c"):
                    needed[d] = True
        return needed

    def finish(self, final_chans):
        for c in final_chans:
            self.nc.gpsimd.wait_ge(self.csems[c], self.chan_val[c])


def build_program():
    nc = bass.Bass("TRN2", target_bir_lowering=False)
    x_d = nc.dram_tensor("x", [S, D], F32, kind="ExternalInput").ap()
    wblk_d = nc.dram_tensor("wblk", [NBLK, 128, 4096], F32, kind="ExternalInput").ap()
    cst_d = nc.dram_tensor("cst", [128, CW], F32, kind="ExternalInput").ap()
    out_d = nc.dram_tensor("out", [S, D], F32, kind="ExternalOutput").ap()
    wsc_d = nc.dram_tensor("wsc", [NBLK, 128, 4096], BF16, kind="Internal").ap()

    with contextlib.ExitStack() as es:

        def sb(name, shape, dt):
            return es.enter_context(nc.sbuf_tensor(name, shape, dt))

        banks = [es.enter_context(nc.psum_tensor("bank%d" % i, [128, 512], F32)) for i in range(8)]
        kT = sb("kT", [128, 4, S], BF16)
        Vs = sb("Vs", [128, S // 128, 512], BF16)
        xt = sb("xt", [128, 4, D], F32)
        xn = sb("xn", [128, 4, D], BF16)
        hT = sb("hT", [128, 8, T], BF16)
        mixT = sb("mixT", [128, 8, T], BF16)
        bT = mixT[:, 0:4, :]
        cT = mixT[:, 4:8, :]
        halo = sb("halo", [128, 4, 2], F32)
        byT = sb("byT", [128, 4, T], BF16)
        NYQ = 3
        yq = sb("yq", [128, NYQ, 512], BF16)
        qT = sb("qT", [128, 4, T], BF16)
        big = sb("big", [128, NFC, T], BF16)
        NP = 6
        Pb = sb("Pb", [128, NP, T], BF16)
        onT = sb("onT", [128, 4, T], BF16)
        sqb = sb("sqb", [128, T], BF16)
        NSCR = 7
        SCW = 516
        scr = sb("scr", [128, NSCR, SCW], F32)
        Lacc = sb("Lacc", [128, 2, T], F32)
        ones_f = sb("ones_f", [128, 128], F32)
        cos_t = sb("cos_t", [128, 32, 32], F32)
        sin_t = sb("sin_t", [128, 32, 32], F32)
        wring = sb("wring", [128, NSLOT, 4096], BF16)
        cst = sb("cst_sb", [128, CW], F32)
        ident = sb("ident", [128, 128], BF16)
        ones = sb("ones", [128, 128], BF16)
        tabs = sb("tabs", [128, 4, 4, 64], F32)
        gsw = sb("gsw", [128, 2, 64], F32)
        small = sb("small", [128, 64], F32)
        ss8 = sb("ss8", [128, 2, 8], F32)
        sd8 = sb("sd8", [128, 2, 8], F32)
        rs8 = sb("rs8", [128, 2, 8], F32)
        bigf = big[:].rearrange("p a b -> p (a b)").bitcast(F32)
        setup_f = bigf[:, 0:1024].rearrange("p (a b) -> p a b", b=32)
        setup_i = bigf[:, 1024:2048].bitcast(I32).rearrange("p (a b) -> p a b", b=32)
        KSF = [("big", i) for i in range(0, 4)]
        KSI = [("big", i) for i in range(4, 8)]

        SM_NEGLAM, SM_SN08, SM_E, SM_SS, SM_SD, SM_RS = 0, 1, 2, 8, 16, 24

        def program(S_):
            scr_i = [0]

            def new_scr():
                i = scr_i[0] % NSCR
                scr_i[0] += 1
                return i

            bank_i = [0]

            def new_bank(lo=0, hi=8):
                n = hi - lo
                i = lo + (bank_i[0] % n)
                bank_i[0] += 1
                return i

            def bankbf(i):
                return banks[i][:].bitcast(BF16)

            S_.dma("sp", [lambda: nc.sync.dma_start(out=cst[:], in_=cst_d)], [], ["cst"], "cst")

            def x_load(j, t):
                r0 = j * T + t * 128
                if j == 0:
                    S_.dma("pool", [lambda: nc.gpsimd.dma_start(out=xt[:, t, :], in_=x_d[r0:r0 + 128, :])],
                           [], [("xt", t)], ("xl", t))
                else:
                    S_.dma("sp", [lambda: nc.sync.dma_start(out=xt[:, t, :], in_=x_d[r0:r0 + 128, :])],
                           [], [("xt", t)], ("xl", t))

            xs = Lacc[:].rearrange("p a b -> p (a b)")
            XS_KEYS = [("Lacc", 0), ("Lacc", 1)]

            def stage_norm(j, t):
                r0 = j * T + t * 128
                S_.dma("act", [lambda: nc.scalar.dma_start(out=xs, in_=x_d[r0:r0 + 128, :])], [], XS_KEYS, "xs")
                norm_pre(xs, XS_KEYS, t)

            for t in range(4):
                x_load(0, t)

            S_.op("pool", lambda: nc.gpsimd.memset(setup_f[:, 0:4, :], 0.0), [], KSF)
            idf = setup_f[:, 0:4, :].rearrange("p a b -> p (a b)")
            S_.op("pool", lambda: nc.gpsimd.affine_select(idf, idf, [[-1, 128]], ALU.not_equal, 1.0, base=0,
                                                          channel_multiplier=1), [], KSF)
            S_.op("dve", lambda: nc.vector.tensor_copy(ident[:], idf), KSF, ["ident"])
            S_.op("dve", lambda: nc.vector.memset(ones[:], 1.0), [], ["ones"])
            S_.op("dve", lambda: nc.vector.memset(halo[:], 0.0), [], [("halo", c) for c in range(4)])
            S_.op("dve", lambda: nc.vector.memset(ones_f[:], 1.0), [], ["ones_f"])

            pos_i = setup_i[:, 0, :]
            S_.op("pool", lambda: nc.gpsimd.iota(pos_i, [[128, 32]], base=0, channel_multiplier=1), KSF, KSI)
            pos_f = small[:, 32:64]
            S_.op("dve", lambda: nc.vector.tensor_copy(pos_f, pos_i), KSI, ["pos_f"])
            inv = cst[:, C_INV:C_INV + 32]
            ang = setup_f
            TWO_PI = 2.0 * math.pi

            def trig(dst, dkey, shift):
                S_.op("dve", lambda: nc.vector.tensor_tensor(
                    out=ang, in0=pos_f.unsqueeze(2).to_broadcast([128, 32, 32]),
                    in1=inv.unsqueeze(1).to_broadcast([128, 32, 32]), op=ALU.mult),
                    ["pos_f", "cst"], KSF)
                if shift != 0.0:
                    S_.op("dve", lambda: nc.vector.tensor_scalar(out=ang, in0=ang, scalar1=shift, scalar2=None,
                                                                 op0=ALU.add), KSF, KSF)
                S_.op("dve", lambda: nc.vector.tensor_scalar(out=dst[:], in0=ang, scalar1=1.0 / TWO_PI, scalar2=None,
                                                             op0=ALU.mult), KSF, [dkey])
                S_.op("dve", lambda: nc.vector.tensor_copy(setup_i, dst[:]), [dkey], KSI)
                S_.op("dve", lambda: nc.vector.tensor_copy(dst[:], setup_i), KSI, [dkey])
                S_.op("dve", lambda: nc.vector.scalar_tensor_tensor(out=dst[:], in0=dst[:], scalar=-TWO_PI, in1=ang,
                                                                    op0=ALU.mult, op1=ALU.add),
                      [dkey] + KSF, [dkey])
                S_.op("dve", lambda: nc.vector.tensor_scalar(out=dst[:], in0=dst[:], scalar1=3.1415925, scalar2=-3.1415925,
                                                             op0=ALU.min, op1=ALU.max), [dkey], [dkey])
                S_.op("act", lambda: nc.scalar.activation(out=dst[:], in_=dst[:], func=AF.Sin), [dkey], [dkey])

            trig(sin_t, "sin_t", 0.0)
            trig(cos_t, "cos_t", math.pi / 2.0)

            lamv = cst[:, C_LAM:C_LAM + 256].rearrange("p (a d) -> p a d", a=4)
            lprod = scr[:, 4, 0:128].rearrange("p (a d) -> p a d", a=2)
            S_.op("dve", lambda: nc.vector.tensor_tensor(out=lprod, in0=lamv[:, 0:2, :], in1=lamv[:, 2:4, :], op=ALU.mult),
                  ["cst"], [("scr", 4)])
            S_.op("dve", lambda: nc.vector.tensor_reduce(out=small[:, SM_E:SM_E + 2], in_=lprod, axis=AX.X, op=ALU.add),
                  [("scr", 4)], ["sm_e"])
            S_.op("act", lambda: nc.scalar.activation(out=small[:, SM_E + 2:SM_E + 4], in_=small[:, SM_E:SM_E + 2], func=AF.Exp),
                  ["sm_e"], ["sm_e2"])
            S_.op("dve", lambda: nc.vector.tensor_tensor(out=small[:, SM_E + 4:SM_E + 5], in0=small[:, SM_E + 3:SM_E + 4],
                                                         in1=small[:, SM_E + 2:SM_E + 3], op=ALU.subtract),
                  ["sm_e2"], ["sm_e3"])
            S_.op("dve", lambda: nc.vector.tensor_scalar(out=small[:, SM_NEGLAM:SM_NEGLAM + 1], in0=small[:, SM_E + 4:SM_E + 5],
                                                         scalar1=-LAM_INIT, scalar2=None, op0=ALU.add),
                  ["sm_e3"], ["neglam"])
            S_.op("dve", lambda: nc.vector.tensor_scalar(out=small[:, SM_SN08:SM_SN08 + 1], in0=cst[:, C_SN:C_SN + 1],
                                                         scalar1=1.0 - LAM_INIT, scalar2=None, op0=ALU.mult),
                  ["cst"], ["sn08"])
            for qk, c0 in ((0, C_QN), (1, C_KN)):
                S_.op("dve", lambda qk=qk, c0=c0: nc.vector.tensor_scalar(
                    out=gsw[:, qk, 0:32], in0=cst[:, c0 + 32:c0 + 64], scalar1=-1.0, scalar2=None, op0=ALU.mult),
                    ["cst"], [("gsw", qk, 0)])
                S_.op("dve", lambda qk=qk, c0=c0: nc.vector.tensor_copy(gsw[:, qk, 32:64], cst[:, c0:c0 + 32]),
                      ["cst"], [("gsw", qk, 1)])

            WORDER = [3, 0, 4, 1, 5, 2, 6, 7, 8, 9] + list(range(10, NBLK))
            wseq = [(jj, b) for jj in range(NST) for b in WORDER]
            wptr = [0]
            wissued = [0]
            NSTG = 3
            stg_i = [0]

            def stg_view(hb):
                return Vs[:, 8 + 8 * hb:16 + 8 * hb, :].rearrange("p a b -> p (a b)").bitcast(F32)

            def stg_keys(hb):
                return [("V", kk) for kk in range(8 + 8 * hb, 16 + 8 * hb)]

            def issue(k):
                jj, b = wseq[k]
                sl = k % NSLOT
                if jj > 0:
                    S_.dma("sp", [lambda: nc.sync.dma_start(out=wring[:, sl, :], in_=wsc_d[b])],
                           [("wsc", b)], [("w", sl)], ("wr", sl))
                    return
                for half in range(2):
                    hb = stg_i[0] % NSTG
                    stg_i[0] += 1
                    sv = stg_view(hb)
                    S_.dma("sp", [lambda half=half, sv=sv: nc.sync.dma_start(out=sv, in_=wblk_d[b][:, half * 2048:(half + 1) * 2048])],
                           [], stg_keys(hb), ("stg", hb))
                    dst = wring[:, sl, half * 2048:(half + 1) * 2048]
                    if half == 0:
                        S_.op("act", lambda dst=dst, sv=sv: nc.scalar.copy(out=dst, in_=sv), stg_keys(hb), [("w", sl)])
                    else:
                        S_.op("dve", lambda dst=dst, sv=sv: nc.vector.tensor_copy(dst, sv), stg_keys(hb), [("w", sl)])
                S_.dma("pool", [lambda: nc.gpsimd.dma_start(out=wsc_d[b], in_=wring[:, sl, :])],
                       [("w", sl)], [("wsc", b)], ("cv", b))

            def wload(b):
                k = wptr[0]
                assert wseq[k][1] == b, (wseq[k], b)
                if wseq[k][0] == 0:
                    depth = 1 if (k >= 1 and wseq[k - 1][1] == 12) else 2
                    lim = min(k + depth, NBLK - 1)
                else:
                    lim = k
                while wissued[0] <= lim:
                    issue(wissued[0])
                    wissued[0] += 1
                wptr[0] += 1
                return k % NSLOT

            def norm_pre(src, src_keys, t):
                S_.op("dve", lambda: nc.vector.memset(small[:, SM_SS + t:SM_SS + t + 1], 0.0), [], [("ss", t)])
                S_.op("act", lambda: nc.scalar.activation(out=xn[:, t, :], in_=src, func=AF.Square,
                                                          accum_out=small[:, SM_SS + t:SM_SS + t + 1]),
                      list(src_keys) + [("ss", t)], [("xn", t), ("ss", t)])
                S_.op("act", lambda: nc.scalar.activation(out=small[:, SM_SD + t:SM_SD + t + 1],
                                                          in_=small[:, SM_SS + t:SM_SS + t + 1], func=AF.Ln,
                                                          bias=EPS, scale=1.0 / D),
                      [("ss", t)], [("sd", t)])
                S_.op("act", lambda: nc.scalar.activation(out=small[:, SM_RS + t:SM_RS + t + 1],
                                                          in_=small[:, SM_SD + t:SM_SD + t + 1], func=AF.Exp, scale=-0.5),
                      [("sd", t)], [("rs", t)])
                S_.op("dve", lambda: nc.vector.tensor_scalar(out=xn[:, t, :], in0=src,
                                                             scalar1=small[:, SM_RS + t:SM_RS + t + 1], scalar2=None,
                                                             op0=ALU.mult),
                      list(src_keys) + [("rs", t)], [("xn", t)])

            def norm_T(t, gcol0, bk=None):
                if bk is None:
                    bk = new_bank()
                bv = bankbf(bk).rearrange("p (c k) -> p c k", c=8)
                for kc in range(8):
                    S_.op("pe", lambda kc=kc: nc.tensor.transpose(bv[:, kc, :], xn[:, t, kc * 128:(kc + 1) * 128], ident[:]),
                          [("xn", t), "ident"], [("ps", bk)])
                S_.op("dve", lambda: nc.vector.tensor_tensor(
                    out=hT[:, :, t * 128:(t + 1) * 128], in0=bv,
                    in1=cst[:, gcol0:gcol0 + 8].unsqueeze(2).to_broadcast([128, 8, 128]), op=ALU.mult),
                    [("ps", bk), "cst"], [("hT", t)])

            HT_ALL = [("hT", t) for t in range(4)]
            QT_ALL = [("qT", t) for t in range(4)]

            def rstd_small(src, dst, tmp, scale, rk, wk, tk):
                S_.op("act", lambda: nc.scalar.activation(out=tmp, in_=src, func=AF.Ln, bias=EPS, scale=scale), [rk], [tk])
                S_.op("act", lambda: nc.scalar.activation(out=dst, in_=tmp, func=AF.Exp, scale=-0.5), [tk], [wk])

            def fm_block(j, b):
                s = wload(b)
                wv = wring[:, s, :].rearrange("p (k c) -> p k c", k=8)
                for c4 in range(4):
                    bk = new_bank()
                    for kc in range(8):
                        S_.op("pe", lambda kc=kc: nc.tensor.matmul(
                            banks[bk][:], lhsT=wv[:, kc, c4 * 128:(c4 + 1) * 128], rhs=hT[:, kc, :],
                            start=(kc == 0), stop=(kc == 7)),
                            [("w", s)] + HT_ALL, [("ps", bk)])
                    if b == 0:
                        S_.op("act", lambda: nc.scalar.copy(out=bT[:, c4, :], in_=banks[bk][:]),
                              [("ps", bk)], [("mixT", c4)])
                    elif b == 1:
                        S_.op("act", lambda: nc.scalar.copy(out=cT[:, c4, :], in_=banks[bk][:]),
                              [("ps", bk)], [("mixT", 4 + c4)])
                    elif b == 2:
                        u = new_scr()
                        S_.op("dve", lambda: nc.vector.tensor_copy(scr[:, u, 0:2], halo[:, c4, :]),
                              [("halo", c4)], [("scr", u)])
                        S_.op("dve", lambda: nc.vector.tensor_tensor(
                            out=scr[:, u, 2:T + 2], in0=banks[bk][:], in1=cT[:, c4, :], op=ALU.mult),
                            [("ps", bk), ("mixT", 4 + c4)], [("scr", u)])
                        S_.op("dve", lambda: nc.vector.tensor_copy(halo[:, c4, :], scr[:, u, T:T + 2]),
                              [("scr", u)], [("halo", c4)])
                        y = new_scr()
                        cw = C_CW + c4 * 3
                        S_.op("dve", lambda: nc.vector.tensor_scalar(
                            out=scr[:, y, 0:T], in0=scr[:, u, 2:T + 2], scalar1=cst[:, cw + 2:cw + 3],
                            scalar2=None, op0=ALU.mult), [("scr", u), "cst"], [("scr", y)])
                        S_.op("dve", lambda: nc.vector.scalar_tensor_tensor(
                            out=scr[:, y, 0:T], in0=scr[:, u, 1:T + 1], scalar=cst[:, cw + 1:cw + 2],
                            in1=scr[:, y, 0:T], op0=ALU.mult, op1=ALU.add), [("scr", u), "cst", ("scr", y)], [("scr", y)])
                        S_.op("dve", lambda: nc.vector.scalar_tensor_tensor(
                            out=scr[:, y, 0:T], in0=scr[:, u, 0:T], scalar=cst[:, cw:cw + 1],
                            in1=scr[:, y, 0:T], op0=ALU.mult, op1=ALU.add), [("scr", u), "cst", ("scr", y)], [("scr", y)])
                        S_.op("dve", lambda: nc.vector.tensor_tensor(
                            out=byT[:, c4, :], in0=scr[:, y, 0:T], in1=bT[:, c4, :], op=ALU.mult),
                            [("scr", y), ("mixT", c4)], [("byT", c4)])
                    else:
                        gi = (b - 6) * 4 + c4
                        S_.op("act", lambda: nc.scalar.activation(
                            out=big[:, gi, :], in_=banks[bk][:], func=AF.Sigmoid,
                            bias=cst[:, C_BG + gi:C_BG + gi + 1], scale=1.0),
                            [("ps", bk), "cst"], [("big", gi)])

            def tm_matmuls(j, b, s, t):
                wv = wring[:, s, :].rearrange("p (k c) -> p k c", k=8)
                bk = new_bank()
                for kc in range(8):
                    S_.op("pe", lambda kc=kc: nc.tensor.matmul(
                        banks[bk][:], lhsT=hT[:, kc, t * 128:(t + 1) * 128], rhs=wv[:, kc, :],
                        start=(kc == 0), stop=(kc == 7)),
                        [("w", s), ("hT", t)], [("ps", bk)])
                return bk

            def qk_chain_stages(j, qk, t, bk):
                yi = (qk * 4 + t) % NYQ
                z, t2 = new_scr(), new_scr()
                z3 = scr[:, z, 0:512].rearrange("p (g d) -> p g d", g=8)
                t23 = scr[:, t2, 0:512].rearrange("p (g d) -> p g d", g=8)
                pb = (qk * 4 + t) % 2

                def sA():
                    S_.op("act", lambda: nc.scalar.copy(out=scr[:, z, 0:512], in_=banks[bk][:]), [("ps", bk)], [("scr", z)])
                    S_.op("dve", lambda: nc.vector.tensor_tensor(out=scr[:, t2, 0:512], in0=scr[:, z, 0:512], in1=scr[:, z, 0:512],
                                                                 op=ALU.mult), [("scr", z)], [("scr", t2)])
                    S_.op("dve", lambda: nc.vector.tensor_reduce(out=ss8[:, pb, :], in_=t23, axis=AX.X, op=ALU.add),
                          [("scr", t2)], [("ss8", pb)])

                def sB():
                    rstd_small(ss8[:, pb, :], rs8[:, pb, :], sd8[:, pb, :], 1.0 / 64, ("ss8", pb), ("rs8", pb), ("sd8", pb))
                    S_.op("pool", lambda: nc.gpsimd.tensor_tensor(
                        out=t23[:, :, 0:32], in0=z3[:, :, 32:64],
                        in1=tabs[:, 2 * qk + 1, t, 0:32].unsqueeze(1).to_broadcast([128, 8, 32]), op=ALU.mult),
                        [("scr", z), ("tab", 2 * qk + 1, t)], [("scr", t2)])
                    S_.op("pool", lambda: nc.gpsimd.tensor_tensor(
                        out=t23[:, :, 32:64], in0=z3[:, :, 0:32],
                        in1=tabs[:, 2 * qk + 1, t, 32:64].unsqueeze(1).to_broadcast([128, 8, 32]), op=ALU.mult),
                        [("scr", z), ("tab", 2 * qk + 1, t)], [("scr", t2)])
                    S_.op("pool", lambda: nc.gpsimd.tensor_tensor(
                        out=z3, in0=z3, in1=tabs[:, 2 * qk, t, :].unsqueeze(1).to_broadcast([128, 8, 64]), op=ALU.mult),
                        [("scr", z), ("tab", 2 * qk, t)], [("scr", z)])
                    S_.op("pool", lambda: nc.gpsimd.tensor_tensor(
                        out=scr[:, z, 0:512], in0=scr[:, z, 0:512], in1=scr[:, t2, 0:512], op=ALU.add),
                        [("scr", z), ("scr", t2)], [("scr", z)])

                def sC():
                    S_.op("dve", lambda: nc.vector.tensor_tensor(
                        out=yq[:, yi, :].rearrange("p (g d) -> p g d", g=8), in0=z3,
                        in1=rs8[:, pb, :].unsqueeze(2).to_broadcast([128, 8, 64]), op=ALU.mult),
                        [("scr", z), ("rs8", pb)], [("yq", yi)])

                return {"qk": qk, "t": t, "yi": yi, "stages": [sA, sB, sC], "age": 0}

            def qk_transpose(j, qk, t, yi):
                bk2 = new_bank()
                bv = bankbf(bk2)[:, 0:512].rearrange("p (c k) -> p c k", c=4)
                for h in range(4):
                    S_.op("pe", lambda h=h: nc.tensor.transpose(
                        bv[:, h, :], yq[:, yi, h * 128:(h + 1) * 128], ident[:]),
                        [("yq", yi), "ident"], [("ps", bk2)])
                if qk == 0:
                    S_.op("act", lambda: nc.scalar.copy(out=qT[:, :, t * 128:(t + 1) * 128], in_=bv),
                          [("ps", bk2)], [("qT", t)])
                else:
                    c0 = j * T + t * 128
                    S_.op("act", lambda: nc.scalar.copy(out=kT[:, :, c0:c0 + 128], in_=bv),
                          [("ps", bk2)], [("kT", j * 4 + t)])

            def phaseB(j):
                pipe = []
                pend = []

                def advance():
                    while pend and pend[0]["age"] >= 2:
                        c = pend.pop(0)
                        qk_transpose(j, c["qk"], c["t"], c["yi"])
                    for c in pend:
                        c["age"] += 1
                    for c in list(pipe):
                        if len(c["stages"]) == 1:
                            for p_ in [p_ for p_ in pend if p_["yi"] == c["yi"]]:
                                pend.remove(p_)
                                qk_transpose(j, p_["qk"], p_["t"], p_["yi"])
                        c["stages"].pop(0)()
                        if not c["stages"]:
                            pipe.remove(c)
                            pend.append(c)

                def tm_block(b, qk):
                    sl = wload(b)
                    for t in range(4):
                        bk = tm_matmuls(j, b, sl, t)
                        if qk is None:
                            S_.op("act", lambda t=t, bk=bk: nc.scalar.copy(out=Vs[:, j * 4 + t, :], in_=banks[bk][:]),
                                  [("ps", bk)], [("V", j * 4 + t)])
                        else:
                            pipe.append(qk_chain_stages(j, qk, t, bk))
                        advance()

                def fm(b):
                    fm_block(j, b)
                    advance()

                tm_block(3, 0)
                fm(0)
                tm_block(4, 1)
                fm(1)
                tm_block(5, None)
                for b in (2, 6, 7, 8, 9):
                    fm(b)
                while pipe or pend:
                    advance()

            def conv_out_part(j, s_co, dcs):
                wco = wring[:, s_co, :].rearrange("p (k c) -> p k c", k=4)
                for dc in dcs:
                    b1 = new_bank(4, 8)
                    for kc in range(4):
                        S_.op("pe", lambda kc=kc: nc.tensor.matmul(
                            banks[b1][:], lhsT=wco[:, kc, dc * 128:(dc + 1) * 128], rhs=byT[:, kc, :],
                            start=(kc == 0), stop=(kc == 3)), [("w", s_co), ("byT", kc)], [("ps", b1)])
                    S_.op("dve", lambda: nc.vector.tensor_tensor(
                        out=mixT[:, dc, :], in0=banks[b1][:], in1=big[:, dc, :], op=ALU.mult),
                        [("ps", b1), ("big", dc)], [("mixT", dc)])

            def attention(j, s_co):
                nkt = 4 * j + 4
                pending = []
                B_L1, B_EP = 2, 3

                def make_epilogue(h):
                    os0, os1, r0, r1 = new_scr(), new_scr(), new_scr(), new_scr()

                    def e1():
                        S_.op("dve", lambda: nc.vector.tensor_copy(scr[:, os0, 0:T], banks[0][:]), [("ps", 0)], [("scr", os0)])
                        S_.op("dve", lambda: nc.vector.tensor_copy(scr[:, os1, 0:T], banks[1][:]), [("ps", 1)], [("scr", os1)])
                        S_.op("act", lambda: nc.scalar.activation(out=scr[:, r1, 0:T], in_=banks[B_L1][:], func=AF.Ln),
                              [("ps", B_L1)], [("scr", r1)])

                    def s1():
                        S_.op("pe", lambda: nc.tensor.matmul(banks[B_EP][:], lhsT=ones_f[:], rhs=Lacc[:, 0, :],
                                                             start=True, stop=True),
                              ["ones_f", ("Lacc", 0)], [("ps", B_EP)])

                    def s2():
                        S_.op("act", lambda: nc.scalar.activation(out=scr[:, r1, 0:T], in_=scr[:, r1, 0:T], func=AF.Exp, scale=-1.0),
                              [("scr", r1)], [("scr", r1)])
                        S_.op("act", lambda: nc.scalar.activation(out=scr[:, r0, 0:T], in_=banks[B_EP][:], func=AF.Ln),
                              [("ps", B_EP)], [("scr", r0)])
                        S_.op("act", lambda: nc.scalar.activation(out=scr[:, r0, 0:T], in_=scr[:, r0, 0:T], func=AF.Exp, scale=-1.0),
                              [("scr", r0)], [("scr", r0)])

                    def s3():
                        S_.op("dve", lambda: nc.vector.tensor_tensor(out=scr[:, os0, 0:T], in0=scr[:, os0, 0:T], in1=scr[:, r0, 0:T], op=ALU.mult),
                              [("scr", os0), ("scr", r0)], [("scr", os0)])
                        S_.op("dve", lambda: nc.vector.tensor_tensor(out=scr[:, os1, 0:T], in0=scr[:, os1, 0:T], in1=scr[:, r1, 0:T], op=ALU.mult),
                              [("scr", os1), ("scr", r1)], [("scr", os1)])
                        S_.op("dve", lambda: nc.vector.scalar_tensor_tensor(
                            out=scr[:, os0, 0:T], in0=scr[:, os1, 0:T], scalar=small[:, SM_NEGLAM:SM_NEGLAM + 1], in1=scr[:, os0, 0:T],
                            op0=ALU.mult, op1=ALU.add), [("scr", os1), ("scr", os0), "neglam"], [("scr", os0)])

                    def s4():
                        S_.op("act", lambda: nc.scalar.activation(out=sqb[:], in_=scr[:, os0, 0:T], func=AF.Square),
                              [("scr", os0)], ["sqb"])

                    def s5():
                        S_.op("pe", lambda: nc.tensor.matmul(banks[B_EP][:], lhsT=ones[:], rhs=sqb[:], start=True, stop=True),
                              ["ones", "sqb"], [("ps", B_EP)])

                    def s6():
                        S_.op("act", lambda: nc.scalar.activation(out=scr[:, r0, 0:T], in_=banks[B_EP][:], func=AF.Ln,
                                                                  bias=EPS, scale=1.0 / 128), [("ps", B_EP)], [("scr", r0)])
                        S_.op("act", lambda: nc.scalar.activation(out=scr[:, r0, 0:T], in_=scr[:, r0, 0:T], func=AF.Exp, scale=-0.5),
                              [("scr", r0)], [("scr", r0)])

                    def s7():
                        S_.op("dve", lambda: nc.vector.scalar_tensor_tensor(
                            out=onT[:, h, :], in0=scr[:, os0, 0:T], scalar=small[:, SM_SN08:SM_SN08 + 1], in1=scr[:, r0, 0:T],
                            op0=ALU.mult, op1=ALU.mult), [("scr", os0), ("scr", r0), "sn08"], [("onT", h)])

                    return [e1, s1, s2, s3, s4, s5, s6, s7]

                for h in range(4):
                    info = {}

                    def emit_qk(g):
                        kt = g
                        r = kt - 4 * j
                        qlo = 128 * r if r > 0 else 0
                        N = T - qlo
                        sb0 = 4 + (g % 2) * 2
                        info[g] = (qlo, N, sb0)
                        for rep in range(1 + NDUP):
                            for n in range(2):
                                S_.op("pe", lambda n=n: nc.tensor.matmul(
                                    banks[sb0 + n][:, 0:N], lhsT=kT[n * 64:(n + 1) * 64, h, kt * 128:(kt + 1) * 128],
                                    rhs=qT[n * 64:(n + 1) * 64, h, qlo:T], start=True, stop=True),
                                    [("kT", kt)] + QT_ALL[qlo // 128:], [("ps", sb0 + n)])

                    emit_qk(0)
                    if nkt > 1:
                        emit_qk(1)
                    for g in range(nkt):
                        if pending:
                            pending.pop(0)()
                        qlo, N, sb0 = info[g]
                        kt = g
                        pis = [(g % 3) * 2 + n for n in range(2)]
                        for n in range(2):
                            pi = pis[n]
                            S_.op("act", lambda n=n, pi=pi: nc.scalar.activation(
                                out=Pb[:, pi, 0:N], in_=banks[sb0 + n][:, 0:N], func=AF.Exp, scale=0.125),
                                [("ps", sb0 + n)], [("P", pi)])
                            if kt >= 4 * j:
                                S_.op("pool", lambda pi=pi: nc.gpsimd.memset(Pb[64:128, pi, 0:64], 0.0), [], [("P", pi)])
                        if g + 2 < nkt:
                            emit_qk(g + 2)
                        first, last = (kt == 0), (kt == nkt - 1)
                        PK = [("P", pis[0]), ("P", pis[1])]
                        for n in range(2):
                            pi = pis[n]
                            S_.op("pe", lambda n=n, pi=pi: nc.tensor.matmul(
                                banks[n][:, qlo:T], lhsT=Vs[:, kt, h * 128:(h + 1) * 128], rhs=Pb[:, pi, 0:N],
                                start=first, stop=last),
                                [("V", kt)] + PK, [("ps", n)])
                        p0, p1 = pis
                        S_.op("pe", lambda p1=p1: nc.tensor.matmul(
                            banks[B_L1][:, qlo:T], lhsT=ones[:], rhs=Pb[:, p1, 0:N], start=first, stop=last),
                            ["ones"] + PK, [("ps", B_L1)])
                        if g == 0:
                            S_.op("dve", lambda p0=p0: nc.vector.tensor_copy(Lacc[:, 0, :], Pb[:, p0, :]),
                                  [("P", p0)], [("Lacc", 0)])
                        else:
                            S_.op("dve", lambda p0=p0: nc.vector.tensor_tensor(
                                out=Lacc[:, 0, qlo:T], in0=Lacc[:, 0, qlo:T], in1=Pb[:, p0, 0:N], op=ALU.add),
                                [("P", p0), ("Lacc", 0)], [("Lacc", 0)])
                    while pending:
                        pending.pop(0)()
                    ep = make_epilogue(h)
                    ep[0]()
                    pending = ep[1:]
                for dc in range(8):
                    conv_out_part(j, s_co, [dc])
                    if pending:
                        pending.pop(0)()
                while pending:
                    pending.pop(0)()

            for j in range(NST):
                if j == 0:
                    for t in range(4):
                        norm_pre(xt[:, t, :], [("xt", t)], t)
                    for t in range(4):
                        norm_T(t, C_GMIX)

                for qk, c0 in ((0, C_QN), (1, C_KN)):
                    for tt in range(4):
                        for half in range(2):
                            S_.op("dve", lambda qk=qk, c0=c0, tt=tt, half=half: nc.vector.tensor_tensor(
                                out=tabs[:, 2 * qk, tt, half * 32:(half + 1) * 32], in0=cos_t[:, j * 4 + tt, :],
                                in1=cst[:, c0 + half * 32:c0 + (half + 1) * 32], op=ALU.mult),
                                ["cos_t", "cst"], [("tab", 2 * qk, tt)])
                            S_.op("dve", lambda qk=qk, tt=tt, half=half: nc.vector.tensor_tensor(
                                out=tabs[:, 2 * qk + 1, tt, half * 32:(half + 1) * 32], in0=sin_t[:, j * 4 + tt, :],
                                in1=gsw[:, qk, half * 32:(half + 1) * 32], op=ALU.mult),
                                ["sin_t", ("gsw", qk, 0), ("gsw", qk, 1)], [("tab", 2 * qk + 1, tt)])

                phaseB(j)

                s_co = wload(10)
                if j > 0:
                    for t in range(4):
                        x_load(j, t)
                attention(j, s_co)

                s_ao = wload(11)
                wao = wring[:, s_ao, :].rearrange("p (k c) -> p k c", k=4)
                for dc in range(8):
                    b2 = new_bank()
                    for kc in range(4):
                        S_.op("pe", lambda kc=kc, dc=dc, b2=b2: nc.tensor.matmul(
                            banks[b2][:], lhsT=wao[:, kc, dc * 128:(dc + 1) * 128], rhs=onT[:, kc, :],
                            start=(kc == 0), stop=(kc == 3)), [("w", s_ao), ("onT", kc)], [("ps", b2)])
                    m2 = new_scr()
                    S_.op("dve", lambda dc=dc, b2=b2, m2=m2: nc.vector.tensor_tensor(
                        out=scr[:, m2, 0:T], in0=banks[b2][:], in1=big[:, 8 + dc, :], op=ALU.mult),
                        [("ps", b2), ("big", 8 + dc)], [("scr", m2)])
                    S_.op("dve", lambda dc=dc, m2=m2: nc.vector.tensor_tensor(
                        out=mixT[:, dc, :], in0=scr[:, m2, 0:T], in1=mixT[:, dc, :], op=ALU.add),
                        [("scr", m2), ("mixT", dc)], [("mixT", dc)])


                MIX_ALL = [("mixT", dc) for dc in range(8)]
                s_o = [wload(12), wload(13)]
                for t in range(4):
                    for c in range(2):
                        wv = wring[:, s_o[c], :].rearrange("p (k c) -> p k c", k=8)
                        bk = new_bank()
                        for kc in range(8):
                            S_.op("pe", lambda kc=kc, t=t, bk=bk, wv=wv: nc.tensor.matmul(
                                banks[bk][:], lhsT=mixT[:, kc, t * 128:(t + 1) * 128], rhs=wv[:, kc, :],
                                start=(kc == 0), stop=(kc == 7)), [("w", s_o[c])] + MIX_ALL, [("ps", bk)])
                        S_.op("dve", lambda t=t, bk=bk, c=c: nc.vector.tensor_tensor(
                            out=xt[:, t, c * 512:(c + 1) * 512], in0=banks[bk][:], in1=xt[:, t, c * 512:(c + 1) * 512],
                            op=ALU.add), [("ps", bk), ("xt", t)], [("xt", t)])
                    norm_pre(xt[:, t, :], [("xt", t)], t)
                    if t > 0:
                        norm_T(t - 1, C_GFFN)
                norm_T(3, C_GFFN)

                for jj in range(11):
                    s = wload(14 + jj)
                    wv = wring[:, s, :].rearrange("p (k c) -> p k c", k=8)
                    for c2 in range(2):
                        fi = jj * 2 + c2
                        bg = new_bank()
                        for kc in range(8):
                            S_.op("pe", lambda kc=kc, c2=c2, bg=bg, wv=wv: nc.tensor.matmul(
                                banks[bg][:], lhsT=wv[:, kc, c2 * 128:(c2 + 1) * 128], rhs=hT[:, kc, :],
                                start=(kc == 0), stop=(kc == 7)), [("w", s)] + HT_ALL, [("ps", bg)])
                        bu = new_bank()
                        for kc in range(8):
                            S_.op("pe", lambda kc=kc, c2=c2, bu=bu, wv=wv: nc.tensor.matmul(
                                banks[bu][:], lhsT=wv[:, kc, 256 + c2 * 128:256 + (c2 + 1) * 128], rhs=hT[:, kc, :],
                                start=(kc == 0), stop=(kc == 7)), [("w", s)] + HT_ALL, [("ps", bu)])
                        sg = new_scr()
                        S_.op("act", lambda bg=bg, sg=sg: nc.scalar.activation(out=scr[:, sg, 0:T], in_=banks[bg][:], func=AF.Silu),
                              [("ps", bg)], [("scr", sg)])
                        S_.op("dve", lambda bu=bu, sg=sg, fi=fi: nc.vector.tensor_tensor(
                            out=big[:, fi, :], in0=banks[bu][:], in1=scr[:, sg, 0:T], op=ALU.mult),
                            [("ps", bu), ("scr", sg)], [("big", fi)])

                for i in range(6):
                    s = wload(25 + i)
                    wv = wring[:, s, :].rearrange("p (k c) -> p k c", k=4)
                    for kl in range(4):
                        kc = 4 * i + kl
                        if kc >= NFC:
                            break
                        for t in range(4):
                            for c in range(2):
                                bk = t * 2 + c
                                S_.op("pe", lambda kc=kc, kl=kl, t=t, c=c, bk=bk, wv=wv: nc.tensor.matmul(
                                    banks[bk][:], lhsT=big[:, kc, t * 128:(t + 1) * 128], rhs=wv[:, kl, c * 512:(c + 1) * 512],
                                    start=(kc == 0), stop=(kc == NFC - 1)), [("w", s), ("big", kc)], [("ps", bk)])
                    if j + 1 < NST and 1 <= i <= 4:
                        stage_norm(j + 1, i - 1)
                for t in range(4):
                    for c in range(2):
                        bk = t * 2 + c
                        S_.op("dve", lambda t=t, c=c, bk=bk: nc.vector.tensor_tensor(
                            out=xt[:, t, c * 512:(c + 1) * 512], in0=banks[bk][:], in1=xt[:, t, c * 512:(c + 1) * 512],
                            op=ALU.add), [("ps", bk), ("xt", t)], [("xt", t)])
                    r0 = j * T + t * 128
                    S_.dma("act", [lambda t=t, r0=r0: nc.scalar.dma_start(out=out_d[r0:r0 + 128, :], in_=xt[:, t, :])],
                           [("xt", t)], [("out", j, t)], ("st", t))
                    if j + 1 < NST:
                        norm_T(t, C_GMIX, bk=2 * t)

        an = Sched(nc, es, needed=None)
        program(an)
        needed = an.analyze()
        em = Sched(nc, es, needed=needed, chans=list(an.chan_val.keys()))
        program(em)
        em.finish([("st", t) for t in range(4)])
    return nc


def _prep_weights(w_in, w_conv_out, w_attn_out, w_o, w_gate_up, w_down):
    blk = np.zeros((NBLK, 128, 4096), dtype=np.float32)

    def put(i, w2d):
        K, C = w2d.shape
        a = w2d.reshape(K // 128, 128, C).transpose(1, 0, 2).reshape(128, -1)
        blk[i, :, :a.shape[1]] = a

    for b in range(10):
        put(b, w_in[:, b * 512:(b + 1) * 512])
    put(10, w_conv_out)
    put(11, w_attn_out)
    for c in range(2):
        put(12 + c, w_o[:, c * 512:(c + 1) * 512])
    for jj in range(11):
        put(14 + jj, np.concatenate([w_gate_up[:, jj * 256:(jj + 1) * 256],
                                     w_gate_up[:, DFF + jj * 256:DFF + (jj + 1) * 256]], axis=1))
    for i in range(6):
        put(25 + i, w_down[i * 512:min((i + 1) * 512, DFF), :])
    return blk


def _prep_consts(g_mix, g_ffn, b_gate, conv_w, q_norm, k_norm, lq1, lk1, lq2, lk2, sub_norm):
    c = np.zeros((128, CW), dtype=np.float32)
    c[:, C_GMIX:C_GMIX + 8] = g_mix.reshape(8, 128).T
    c[:, C_GFFN:C_GFFN + 8] = g_ffn.reshape(8, 128).T
    c[:, C_BG:C_BG + 16] = b_gate.reshape(16, 128).T
    c[:, C_CW:C_CW + 12] = conv_w.reshape(3, 4, 128).transpose(2, 1, 0).reshape(128, 12)
    c[:, C_QN:C_QN + 64] = q_norm[None, :]
    c[:, C_KN:C_KN + 64] = k_norm[None, :]
    c[:, C_LAM:C_LAM + 256] = np.concatenate([lq1, lq2, lk1, lk2])[None, :]
    c[:, C_SN] = sub_norm
    inv = (1.0 / (np.float32(10000.0) ** (np.arange(0, 64, 2, dtype=np.float32) / np.float32(64)))).astype(np.float32)
    c[:, C_INV:C_INV + 32] = inv[None, :]
    return c


_NC_CACHE = {}


def kernel(x, g_mix, w_in, b_gate, conv_w, q_norm, k_norm, lambda_q1, lambda_k1, lambda_q2, lambda_k2,
           sub_norm, w_conv_out, w_attn_out, w_o, g_ffn, w_gate_up, w_down):
    f = lambda a: np.ascontiguousarray(np.asarray(a, dtype=np.float32))
    x = f(x)
    wblk = _prep_weights(f(w_in)[0], f(w_conv_out)[0], f(w_attn_out)[0], f(w_o)[0], f(w_gate_up)[0], f(w_down)[0])
    cst = _prep_consts(f(g_mix)[0], f(g_ffn)[0], f(b_gate)[0], f(conv_w)[0], f(q_norm)[0], f(k_norm)[0],
                       f(lambda_q1)[0], f(lambda_k1)[0], f(lambda_q2)[0], f(lambda_k2)[0], f(sub_norm)[0])
    if "nc" not in _NC_CACHE:
        _NC_CACHE["nc"] = build_program()
    nc = _NC_CACHE["nc"]
    in_maps = [{"x": x[c], "wblk": wblk, "cst": cst} for c in range(NCORES)]
    res = run_bass_kernel_spmd(nc, in_maps, core_ids=list(range(NCORES)))
    out = np.stack([np.asarray(res.results[c]["out"], dtype=np.float32) for c in range(NCORES)], axis=0)
    return out
```

```python
import contextlib
import math

import numpy as np
import concourse.bass as bass
import concourse.mybir as mybir
from concourse.bass_utils import run_bass_kernel_spmd

F32 = mybir.dt.float32
BF16 = mybir.dt.bfloat16
I32 = mybir.dt.int32
AF = mybir.ActivationFunctionType
ALU = mybir.AluOpType
AX = mybir.AxisListType

D = 1024
S = 4096
NCORES = 8
T = 512
NST = S // T
DFF = 2816
NFC = DFF // 128
EPS = 1e-6
LAM_INIT = 0.8 - 0.6 * math.exp(0.0)
NBLK = 31
NSLOT = 3
LOOKAHEAD = 2
NDUP = 0

C_GMIX, C_GFFN, C_BG, C_CW, C_QN, C_KN, C_LAM, C_SN, C_INV, C_END = 0, 8, 16, 32, 44, 108, 172, 428, 429, 461
CW = 464


class Sched:
    def __init__(self, nc, es, needed=None, chans=()):
        self.nc = nc
        self.es = es
        self.emit_mode = needed is not None
        self.needed = needed
        self.ops = []
        self.deps_all = []
        self.lastw = {}
        self.readers = {}
        self.engs = {"pe": nc.tensor, "act": nc.scalar, "dve": nc.vector, "pool": nc.gpsimd, "sp": nc.sync}
        self.chan_val = {}
        if self.emit_mode:
            self.sems = {e: es.enter_context(nc.semaphore("sem_" + e)) for e in ("pe", "act", "dve", "pool")}
            self.csems = {c: es.enter_context(nc.semaphore("ch_" + "_".join(str(k) for k in (c if isinstance(c, tuple) else (c,)))))
                          for c in chans}
            self.counts = {e: 0 for e in self.sems}
            self.seen = {e: {} for e in self.engs}

    def _deps(self, reads, writes):
        deps = set()
        for r in reads:
            w = self.lastw.get(r)
            if w is not None:
                deps.add(w)
        for w_ in writes:
            w = self.lastw.get(w_)
            if w is not None:
                deps.add(w)
            for rd in self.readers.get(w_, {}).values():
                deps.add(rd)
        return deps

    def _commit(self, idx, skey, reads, writes):
        for r in reads:
            self.readers.setdefault(r, {})[skey] = idx
        for w in writes:
            self.lastw[w] = idx
            self.readers[w] = {}

    def _waits(self, eng_name, kind, deps):
        eng = self.engs[eng_name]
        sn = self.seen[eng_name]
        waits = {}
        for d in deps:
            p_eng, p_kind, p_chan, p_v = self.ops[d]
            if p_kind == "c":
                if p_eng == "pe" and eng_name == "pe" and kind == "c":
                    continue
                key = ("e", p_eng)
            else:
                key = ("c", p_chan)
            if p_v > waits.get(key, -1):
                waits[key] = p_v
        for key, v in waits.items():
            if sn.get(key, -1) >= v:
                continue
            sn[key] = v
            sem = self.sems[key[1]] if key[0] == "e" else self.csems[key[1]]
            eng.wait_ge(sem, v)

    def op(self, eng, fn, reads=(), writes=()):
        deps = self._deps(reads, writes)
        idx = len(self.ops)
        if self.emit_mode:
            self._waits(eng, "c", deps)
            ins = fn()
            cnt = None
            if self.needed[idx]:
                self.counts[eng] += 1
                cnt = self.counts[eng]
                ins.then_inc(self.sems[eng], 1)
            self.ops.append((eng, "c", None, cnt))
        else:
            self.ops.append((eng, "c", None, None))
            self.deps_all.append(deps)
        self._commit(idx, eng, reads, writes)
        return idx

    def dma(self, queue, fns, reads, writes, chan):
        deps = self._deps(reads, writes)
        idx = len(self.ops)
        self.chan_val[chan] = self.chan_val.get(chan, 0) + 16 * len(fns)
        if self.emit_mode:
            self._waits(queue, "d", deps)
            for f in fns:
                f().then_inc(self.csems[chan], 16)
        else:
            self.deps_all.append(deps)
        self.ops.append((queue, "d", chan, self.chan_val[chan]))
        self._commit(idx, ("ch", chan), reads, writes)
        return idx

    def analyze(self):
        needed = [False] * len(self.ops)
        for i, deps in enumerate(self.deps_all):
            o_eng, o_kind, _, _ = self.ops[i]
            for d in deps:
                p_eng, p_kind, _, _ = self.ops[d]
                if p_kind == "c" and not (p_eng == "pe" and o_eng == "pe" and o_kind == "c"):
                    needed[d] = True
        return needed

    def finish(self, final_chans):
        for c in final_chans:
            self.nc.gpsimd.wait_ge(self.csems[c], self.chan_val[c])


def build_program():
    nc = bass.Bass("TRN2", target_bir_lowering=False)
    x_d = nc.dram_tensor("x", [S, D], F32, kind="ExternalInput").ap()
    wblk_d = nc.dram_tensor("wblk", [NBLK, 128, 4096], F32, kind="ExternalInput").ap()
    cst_d = nc.dram_tensor("cst", [128, CW], F32, kind="ExternalInput").ap()
    out_d = nc.dram_tensor("out", [S, D], F32, kind="ExternalOutput").ap()
    wsc_d = nc.dram_tensor("wsc", [NBLK, 128, 4096], BF16, kind="Internal").ap()

    with contextlib.ExitStack() as es:

        def sb(name, shape, dt):
            return es.enter_context(nc.sbuf_tensor(name, shape, dt))

        banks = [es.enter_context(nc.psum_tensor("bank%d" % i, [128, 512], F32)) for i in range(8)]
        kT = sb("kT", [128, 4, S], BF16)
        Vs = sb("Vs", [128, S // 128, 512], BF16)
        xt = sb("xt", [128, 4, D], F32)
        xn = sb("xn", [128, 4, D], BF16)
        hT = sb("hT", [128, 8, T], BF16)
        mixT = sb("mixT", [128, 8, T], BF16)
        bT = mixT[:, 0:4, :]
        cT = mixT[:, 4:8, :]
        halo = sb("halo", [128, 4, 2], F32)
        byT = sb("byT", [128, 4, T], BF16)
        NYQ = 3
        yq = sb("yq", [128, NYQ, 512], BF16)
        qT = sb("qT", [128, 4, T], BF16)
        big = sb("big", [128, NFC, T], BF16)
        NP = 6
        Pb = sb("Pb", [128, NP, T], BF16)
        onT = sb("onT", [128, 4, T], BF16)
        sqb = sb("sqb", [128, T], BF16)
        NSCR = 7
        SCW = 516
        scr = sb("scr", [128, NSCR, SCW], F32)
        Lacc = sb("Lacc", [128, 2, T], F32)
        ones_f = sb("ones_f", [128, 128], F32)
        cos_t = sb("cos_t", [128, 32, 32], F32)
        sin_t = sb("sin_t", [128, 32, 32], F32)
        wring = sb("wring", [128, NSLOT, 4096], BF16)
        cst = sb("cst_sb", [128, CW], F32)
        ident = sb("ident", [128, 128], BF16)
        ones = sb("ones", [128, 128], BF16)
        tabs = sb("tabs", [128, 4, 4, 64], F32)
        gsw = sb("gsw", [128, 2, 64], F32)
        small = sb("small", [128, 64], F32)
        ss8 = sb("ss8", [128, 2, 8], F32)
        sd8 = sb("sd8", [128, 2, 8], F32)
        rs8 = sb("rs8", [128, 2, 8], F32)
        bigf = big[:].rearrange("p a b -> p (a b)").bitcast(F32)
        setup_f = bigf[:, 0:1024].rearrange("p (a b) -> p a b", b=32)
        setup_i = bigf[:, 1024:2048].bitcast(I32).rearrange("p (a b) -> p a b", b=32)
        KSF = [("big", i) for i in range(0, 4)]
        KSI = [("big", i) for i in range(4, 8)]

        SM_NEGLAM, SM_SN08, SM_E, SM_SS, SM_SD, SM_RS = 0, 1, 2, 8, 16, 24

        def program(S_):
            scr_i = [0]

            def new_scr():
                i = scr_i[0] % NSCR
                scr_i[0] += 1
                return i

            bank_i = [0]

            def new_bank(lo=0, hi=8):
                n = hi - lo
                i = lo + (bank_i[0] % n)
                bank_i[0] += 1
                return i

            def bankbf(i):
                return banks[i][:].bitcast(BF16)

            S_.dma("sp", [lambda: nc.sync.dma_start(out=cst[:], in_=cst_d)], [], ["cst"], "cst")

            def x_load(j, t):
                r0 = j * T + t * 128
                if j == 0:
                    S_.dma("pool", [lambda: nc.gpsimd.dma_start(out=xt[:, t, :], in_=x_d[r0:r0 + 128, :])],
                           [], [("xt", t)], ("xl", t))
                else:
                    S_.dma("sp", [lambda: nc.sync.dma_start(out=xt[:, t, :], in_=x_d[r0:r0 + 128, :])],
                           [], [("xt", t)], ("xl", t))

            xs = Lacc[:].rearrange("p a b -> p (a b)")
            XS_KEYS = [("Lacc", 0), ("Lacc", 1)]

            def stage_norm(j, t):
                r0 = j * T + t * 128
                S_.dma("act", [lambda: nc.scalar.dma_start(out=xs, in_=x_d[r0:r0 + 128, :])], [], XS_KEYS, "xs")
                norm_pre(xs, XS_KEYS, t)

            for t in range(4):
                x_load(0, t)

            S_.op("pool", lambda: nc.gpsimd.memset(setup_f[:, 0:4, :], 0.0), [], KSF)
            idf = setup_f[:, 0:4, :].rearrange("p a b -> p (a b)")
            S_.op("pool", lambda: nc.gpsimd.affine_select(idf, idf, [[-1, 128]], ALU.not_equal, 1.0, base=0,
                                                          channel_multiplier=1), [], KSF)
            S_.op("dve", lambda: nc.vector.tensor_copy(ident[:], idf), KSF, ["ident"])
            S_.op("dve", lambda: nc.vector.memset(ones[:], 1.0), [], ["ones"])
            S_.op("dve", lambda: nc.vector.memset(halo[:], 0.0), [], [("halo", c) for c in range(4)])
            S_.op("dve", lambda: nc.vector.memset(ones_f[:], 1.0), [], ["ones_f"])

            pos_i = setup_i[:, 0, :]
            S_.op("pool", lambda: nc.gpsimd.iota(pos_i, [[128, 32]], base=0, channel_multiplier=1), KSF, KSI)
            pos_f = small[:, 32:64]
            S_.op("dve", lambda: nc.vector.tensor_copy(pos_f, pos_i), KSI, ["pos_f"])
            inv = cst[:, C_INV:C_INV + 32]
            ang = setup_f
            TWO_PI = 2.0 * math.pi

            def trig(dst, dkey, shift):
                S_.op("dve", lambda: nc.vector.tensor_tensor(
                    out=ang, in0=pos_f.unsqueeze(2).to_broadcast([128, 32, 32]),
                    in1=inv.unsqueeze(1).to_broadcast([128, 32, 32]), op=ALU.mult),
                    ["pos_f", "cst"], KSF)
                if shift != 0.0:
                    S_.op("dve", lambda: nc.vector.tensor_scalar(out=ang, in0=ang, scalar1=shift, scalar2=None,
                                                                 op0=ALU.add), KSF, KSF)
                S_.op("dve", lambda: nc.vector.tensor_scalar(out=dst[:], in0=ang, scalar1=1.0 / TWO_PI, scalar2=None,
                                                             op0=ALU.mult), KSF, [dkey])
                S_.op("dve", lambda: nc.vector.tensor_copy(setup_i, dst[:]), [dkey], KSI)
                S_.op("dve", lambda: nc.vector.tensor_copy(dst[:], setup_i), KSI, [dkey])
                S_.op("dve", lambda: nc.vector.scalar_tensor_tensor(out=dst[:], in0=dst[:], scalar=-TWO_PI, in1=ang,
                                                                    op0=ALU.mult, op1=ALU.add),
                      [dkey] + KSF, [dkey])
                S_.op("dve", lambda: nc.vector.tensor_scalar(out=dst[:], in0=dst[:], scalar1=3.1415925, scalar2=-3.1415925,
                                                             op0=ALU.min, op1=ALU.max), [dkey], [dkey])
                S_.op("act", lambda: nc.scalar.activation(out=dst[:], in_=dst[:], func=AF.Sin), [dkey], [dkey])

            trig(sin_t, "sin_t", 0.0)
            trig(cos_t, "cos_t", math.pi / 2.0)

            lamv = cst[:, C_LAM:C_LAM + 256].rearrange("p (a d) -> p a d", a=4)
            lprod = scr[:, 4, 0:128].rearrange("p (a d) -> p a d", a=2)
            S_.op("dve", lambda: nc.vector.tensor_tensor(out=lprod, in0=lamv[:, 0:2, :], in1=lamv[:, 2:4, :], op=ALU.mult),
                  ["cst"], [("scr", 4)])
            S_.op("dve", lambda: nc.vector.tensor_reduce(out=small[:, SM_E:SM_E + 2], in_=lprod, axis=AX.X, op=ALU.add),
                  [("scr", 4)], ["sm_e"])
            S_.op("act", lambda: nc.scalar.activation(out=small[:, SM_E + 2:SM_E + 4], in_=small[:, SM_E:SM_E + 2], func=AF.Exp),
                  ["sm_e"], ["sm_e2"])
            S_.op("dve", lambda: nc.vector.tensor_tensor(out=small[:, SM_E + 4:SM_E + 5], in0=small[:, SM_E + 3:SM_E + 4],
                                                         in1=small[:, SM_E + 2:SM_E + 3], op=ALU.subtract),
                  ["sm_e2"], ["sm_e3"])
            S_.op("dve", lambda: nc.vector.tensor_scalar(out=small[:, SM_NEGLAM:SM_NEGLAM + 1], in0=small[:, SM_E + 4:SM_E + 5],
                                                         scalar1=-LAM_INIT, scalar2=None, op0=ALU.add),
                  ["sm_e3"], ["neglam"])
            S_.op("dve", lambda: nc.vector.tensor_scalar(out=small[:, SM_SN08:SM_SN08 + 1], in0=cst[:, C_SN:C_SN + 1],
                                                         scalar1=1.0 - LAM_INIT, scalar2=None, op0=ALU.mult),
                  ["cst"], ["sn08"])
            for qk, c0 in ((0, C_QN), (1, C_KN)):
                S_.op("dve", lambda qk=qk, c0=c0: nc.vector.tensor_scalar(
                    out=gsw[:, qk, 0:32], in0=cst[:, c0 + 32:c0 + 64], scalar1=-1.0, scalar2=None, op0=ALU.mult),
                    ["cst"], [("gsw", qk, 0)])
                S_.op("dve", lambda qk=qk, c0=c0: nc.vector.tensor_copy(gsw[:, qk, 32:64], cst[:, c0:c0 + 32]),
                      ["cst"], [("gsw", qk, 1)])

            WORDER = [3, 0, 4, 1, 5, 2, 6, 7, 8, 9] + list(range(10, NBLK))
            wseq = [(jj, b) for jj in range(NST) for b in WORDER]
            wptr = [0]
            wissued = [0]
            NSTG = 3
            stg_i = [0]

            def stg_view(hb):
                return Vs[:, 8 + 8 * hb:16 + 8 * hb, :].rearrange("p a b -> p (a b)").bitcast(F32)

            def stg_keys(hb):
                return [("V", kk) for kk in range(8 + 8 * hb, 16 + 8 * hb)]

            def issue(k):
                jj, b = wseq[k]
                sl = k % NSLOT
                if jj > 0:
                    S_.dma("sp", [lambda: nc.sync.dma_start(out=wring[:, sl, :], in_=wsc_d[b])],
                           [("wsc", b)], [("w", sl)], ("wr", sl))
                    return
                for half in range(2):
                    hb = stg_i[0] % NSTG
                    stg_i[0] += 1
                    sv = stg_view(hb)
                    S_.dma("sp", [lambda half=half, sv=sv: nc.sync.dma_start(out=sv, in_=wblk_d[b][:, half * 2048:(half + 1) * 2048])],
                           [], stg_keys(hb), ("stg", hb))
                    dst = wring[:, sl, half * 2048:(half + 1) * 2048]
                    if half == 0:
                        S_.op("act", lambda dst=dst, sv=sv: nc.scalar.copy(out=dst, in_=sv), stg_keys(hb), [("w", sl)])
                    else:
                        S_.op("dve", lambda dst=dst, sv=sv: nc.vector.tensor_copy(dst, sv), stg_keys(hb), [("w", sl)])
                S_.dma("pool", [lambda: nc.gpsimd.dma_start(out=wsc_d[b], in_=wring[:, sl, :])],
                       [("w", sl)], [("wsc", b)], ("cv", b))

            def wload(b):
                k = wptr[0]
                assert wseq[k][1] == b, (wseq[k], b)
                if wseq[k][0] == 0:
                    depth = 1 if (k >= 1 and wseq[k - 1][1] == 12) else 2
                    lim = min(k + depth, NBLK - 1)
                else:
                    lim = k
                while wissued[0] <= lim:
                    issue(wissued[0])
                    wissued[0] += 1
                wptr[0] += 1
                return k % NSLOT

            def norm_pre(src, src_keys, t):
                S_.op("dve", lambda: nc.vector.memset(small[:, SM_SS + t:SM_SS + t + 1], 0.0), [], [("ss", t)])
                S_.op("act", lambda: nc.scalar.activation(out=xn[:, t, :], in_=src, func=AF.Square,
                                                          accum_out=small[:, SM_SS + t:SM_SS + t + 1]),
                      list(src_keys) + [("ss", t)], [("xn", t), ("ss", t)])
                S_.op("act", lambda: nc.scalar.activation(out=small[:, SM_SD + t:SM_SD + t + 1],
                                                          in_=small[:, SM_SS + t:SM_SS + t + 1], func=AF.Ln,
                                                          bias=EPS, scale=1.0 / D),
                      [("ss", t)], [("sd", t)])
                S_.op("act", lambda: nc.scalar.activation(out=small[:, SM_RS + t:SM_RS + t + 1],
                                                          in_=small[:, SM_SD + t:SM_SD + t + 1], func=AF.Exp, scale=-0.5),
                      [("sd", t)], [("rs", t)])
                S_.op("dve", lambda: nc.vector.tensor_scalar(out=xn[:, t, :], in0=src,
                                                             scalar1=small[:, SM_RS + t:SM_RS + t + 1], scalar2=None,
                                                             op0=ALU.mult),
                      list(src_keys) + [("rs", t)], [("xn", t)])

            def norm_T(t, gcol0, bk=None):
                if bk is None:
                    bk = new_bank()
                bv = bankbf(bk).rearrange("p (c k) -> p c k", c=8)
                for kc in range(8):
                    S_.op("pe", lambda kc=kc: nc.tensor.transpose(bv[:, kc, :], xn[:, t, kc * 128:(kc + 1) * 128], ident[:]),
                          [("xn", t), "ident"], [("ps", bk)])
                S_.op("dve", lambda: nc.vector.tensor_tensor(
                    out=hT[:, :, t * 128:(t + 1) * 128], in0=bv,
                    in1=cst[:, gcol0:gcol0 + 8].unsqueeze(2).to_broadcast([128, 8, 128]), op=ALU.mult),
                    [("ps", bk), "cst"], [("hT", t)])

            HT_ALL = [("hT", t) for t in range(4)]
            QT_ALL = [("qT", t) for t in range(4)]

            def rstd_small(src, dst, tmp, scale, rk, wk, tk):
                S_.op("act", lambda: nc.scalar.activation(out=tmp, in_=src, func=AF.Ln, bias=EPS, scale=scale), [rk], [tk])
                S_.op("act", lambda: nc.scalar.activation(out=dst, in_=tmp, func=AF.Exp, scale=-0.5), [tk], [wk])

            def fm_block(j, b):
                s = wload(b)
                wv = wring[:, s, :].rearrange("p (k c) -> p k c", k=8)
                for c4 in range(4):
                    bk = new_bank()
                    for kc in range(8):
                        S_.op("pe", lambda kc=kc: nc.tensor.matmul(
                            banks[bk][:], lhsT=wv[:, kc, c4 * 128:(c4 + 1) * 128], rhs=hT[:, kc, :],
                            start=(kc == 0), stop=(kc == 7)),
                            [("w", s)] + HT_ALL, [("ps", bk)])
                    if b == 0:
                        S_.op("act", lambda: nc.scalar.copy(out=bT[:, c4, :], in_=banks[bk][:]),
                              [("ps", bk)], [("mixT", c4)])
                    elif b == 1:
                        S_.op("act", lambda: nc.scalar.copy(out=cT[:, c4, :], in_=banks[bk][:]),
                              [("ps", bk)], [("mixT", 4 + c4)])
                    elif b == 2:
                        u = new_scr()
                        S_.op("dve", lambda: nc.vector.tensor_copy(scr[:, u, 0:2], halo[:, c4, :]),
                              [("halo", c4)], [("scr", u)])
                        S_.op("dve", lambda: nc.vector.tensor_tensor(
                            out=scr[:, u, 2:T + 2], in0=banks[bk][:], in1=cT[:, c4, :], op=ALU.mult),
                            [("ps", bk), ("mixT", 4 + c4)], [("scr", u)])
                        S_.op("dve", lambda: nc.vector.tensor_copy(halo[:, c4, :], scr[:, u, T:T + 2]),
                              [("scr", u)], [("halo", c4)])
                        y = new_scr()
                        cw = C_CW + c4 * 3
                        S_.op("dve", lambda: nc.vector.tensor_scalar(
                            out=scr[:, y, 0:T], in0=scr[:, u, 2:T + 2], scalar1=cst[:, cw + 2:cw + 3],
                            scalar2=None, op0=ALU.mult), [("scr", u), "cst"], [("scr", y)])
                        S_.op("dve", lambda: nc.vector.scalar_tensor_tensor(
                            out=scr[:, y, 0:T], in0=scr[:, u, 1:T + 1], scalar=cst[:, cw + 1:cw + 2],
                            in1=scr[:, y, 0:T], op0=ALU.mult, op1=ALU.add), [("scr", u), "cst", ("scr", y)], [("scr", y)])
                        S_.op("dve", lambda: nc.vector.scalar_tensor_tensor(
                            out=scr[:, y, 0:T], in0=scr[:, u, 0:T], scalar=cst[:, cw:cw + 1],
                            in1=scr[:, y, 0:T], op0=ALU.mult, op1=ALU.add), [("scr", u), "cst", ("scr", y)], [("scr", y)])
                        S_.op("dve", lambda: nc.vector.tensor_tensor(
                            out=byT[:, c4, :], in0=scr[:, y, 0:T], in1=bT[:, c4, :], op=ALU.mult),
                            [("scr", y), ("mixT", c4)], [("byT", c4)])
                    else:
                        gi = (b - 6) * 4 + c4
                        S_.op("act", lambda: nc.scalar.activation(
                            out=big[:, gi, :], in_=banks[bk][:], func=AF.Sigmoid,
                            bias=cst[:, C_BG + gi:C_BG + gi + 1], scale=1.0),
                            [("ps", bk), "cst"], [("big", gi)])

            def tm_matmuls(j, b, s, t):
                wv = wring[:, s, :].rearrange("p (k c) -> p k c", k=8)
                bk = new_bank()
                for kc in range(8):
                    S_.op("pe", lambda kc=kc: nc.tensor.matmul(
                        banks[bk][:], lhsT=hT[:, kc, t * 128:(t + 1) * 128], rhs=wv[:, kc, :],
                        start=(kc == 0), stop=(kc == 7)),
                        [("w", s), ("hT", t)], [("ps", bk)])
                return bk

            def qk_chain_stages(j, qk, t, bk):
                yi = (qk * 4 + t) % NYQ
                z, t2 = new_scr(), new_scr()
                z3 = scr[:, z, 0:512].rearrange("p (g d) -> p g d", g=8)
                t23 = scr[:, t2, 0:512].rearrange("p (g d) -> p g d", g=8)
                pb = (qk * 4 + t) % 2

                def sA():
                    S_.op("act", lambda: nc.scalar.copy(out=scr[:, z, 0:512], in_=banks[bk][:]), [("ps", bk)], [("scr", z)])
                    S_.op("dve", lambda: nc.vector.tensor_tensor(out=scr[:, t2, 0:512], in0=scr[:, z, 0:512], in1=scr[:, z, 0:512],
                                                                 op=ALU.mult), [("scr", z)], [("scr", t2)])
                    S_.op("dve", lambda: nc.vector.tensor_reduce(out=ss8[:, pb, :], in_=t23, axis=AX.X, op=ALU.add),
                          [("scr", t2)], [("ss8", pb)])
                    rstd_small(ss8[:, pb, :], rs8[:, pb, :], sd8[:, pb, :], 1.0 / 64, ("ss8", pb), ("rs8", pb), ("sd8", pb))

                def sB():
                    S_.op("pool", lambda: nc.gpsimd.tensor_tensor(
                        out=t23[:, :, 0:32], in0=z3[:, :, 32:64],
                        in1=tabs[:, 2 * qk + 1, t, 0:32].unsqueeze(1).to_broadcast([128, 8, 32]), op=ALU.mult),
                        [("scr", z), ("tab", 2 * qk + 1, t)], [("scr", t2)])
                    S_.op("pool", lambda: nc.gpsimd.tensor_tensor(
                        out=t23[:, :, 32:64], in0=z3[:, :, 0:32],
                        in1=tabs[:, 2 * qk + 1, t, 32:64].unsqueeze(1).to_broadcast([128, 8, 32]), op=ALU.mult),
                        [("scr", z), ("tab", 2 * qk + 1, t)], [("scr", t2)])
                    S_.op("pool", lambda: nc.gpsimd.tensor_tensor(
                        out=z3, in0=z3, in1=tabs[:, 2 * qk, t, :].unsqueeze(1).to_broadcast([128, 8, 64]), op=ALU.mult),
                        [("scr", z), ("tab", 2 * qk, t)], [("scr", z)])
                    S_.op("pool", lambda: nc.gpsimd.tensor_tensor(
                        out=scr[:, z, 0:512], in0=scr[:, z, 0:512], in1=scr[:, t2, 0:512], op=ALU.add),
                        [("scr", z), ("scr", t2)], [("scr", z)])

                def sC():
                    S_.op("dve", lambda: nc.vector.tensor_tensor(
                        out=yq[:, yi, :].rearrange("p (g d) -> p g d", g=8), in0=z3,
                        in1=rs8[:, pb, :].unsqueeze(2).to_broadcast([128, 8, 64]), op=ALU.mult),
                        [("scr", z), ("rs8", pb)], [("yq", yi)])

                return {"qk": qk, "t": t, "yi": yi, "stages": [sA, sB, sC], "age": 0}

            def qk_transpose(j, qk, t, yi):
                bk2 = new_bank()
                bv = bankbf(bk2)[:, 0:512].rearrange("p (c k) -> p c k", c=4)
                for h in range(4):
                    S_.op("pe", lambda h=h: nc.tensor.transpose(
                        bv[:, h, :], yq[:, yi, h * 128:(h + 1) * 128], ident[:]),
                        [("yq", yi), "ident"], [("ps", bk2)])
                if qk == 0:
                    S_.op("act", lambda: nc.scalar.copy(out=qT[:, :, t * 128:(t + 1) * 128], in_=bv),
                          [("ps", bk2)], [("qT", t)])
                else:
                    c0 = j * T + t * 128
                    S_.op("act", lambda: nc.scalar.copy(out=kT[:, :, c0:c0 + 128], in_=bv),
                          [("ps", bk2)], [("kT", j * 4 + t)])

            def phaseB(j):
                pipe = []
                pend = []

                def advance():
                    while pend and pend[0]["age"] >= 2:
                        c = pend.pop(0)
                        qk_transpose(j, c["qk"], c["t"], c["yi"])
                    for c in pend:
                        c["age"] += 1
                    for c in list(pipe):
                        if len(c["stages"]) == 1:
                            for p_ in [p_ for p_ in pend if p_["yi"] == c["yi"]]:
                                pend.remove(p_)
                                qk_transpose(j, p_["qk"], p_["t"], p_["yi"])
                        c["stages"].pop(0)()
                        if not c["stages"]:
                            pipe.remove(c)
                            pend.append(c)

                def tm_block(b, qk):
                    sl = wload(b)
                    for t in range(4):
                        bk = tm_matmuls(j, b, sl, t)
                        if qk is None:
                            S_.op("act", lambda t=t, bk=bk: nc.scalar.copy(out=Vs[:, j * 4 + t, :], in_=banks[bk][:]),
                                  [("ps", bk)], [("V", j * 4 + t)])
                        else:
                            pipe.append(qk_chain_stages(j, qk, t, bk))
                        advance()

                def fm(b):
                    fm_block(j, b)
                    advance()

                tm_block(3, 0)
                fm(0)
                tm_block(4, 1)
                fm(1)
                tm_block(5, None)
                for b in (2, 6, 7, 8, 9):
                    fm(b)
                while pipe or pend:
                    advance()

            def conv_out_part(j, s_co, dcs):
                wco = wring[:, s_co, :].rearrange("p (k c) -> p k c", k=4)
                for dc in dcs:
                    b1 = new_bank(4, 8)
                    for kc in range(4):
                        S_.op("pe", lambda kc=kc: nc.tensor.matmul(
                            banks[b1][:], lhsT=wco[:, kc, dc * 128:(dc + 1) * 128], rhs=byT[:, kc, :],
                            start=(kc == 0), stop=(kc == 3)), [("w", s_co), ("byT", kc)], [("ps", b1)])
                    S_.op("dve", lambda: nc.vector.tensor_tensor(
                        out=mixT[:, dc, :], in0=banks[b1][:], in1=big[:, dc, :], op=ALU.mult),
                        [("ps", b1), ("big", dc)], [("mixT", dc)])

            def attention(j, s_co):
                nkt = 4 * j + 4
                pending = []
                B_L1, B_EP = 2, 3

                def make_epilogue(h):
                    os0, os1, r0, r1 = new_scr(), new_scr(), new_scr(), new_scr()

                    def e1():
                        S_.op("dve", lambda: nc.vector.tensor_copy(scr[:, os0, 0:T], banks[0][:]), [("ps", 0)], [("scr", os0)])
                        S_.op("dve", lambda: nc.vector.tensor_copy(scr[:, os1, 0:T], banks[1][:]), [("ps", 1)], [("scr", os1)])

                    def sL():
                        S_.op("act", lambda: nc.scalar.activation(out=scr[:, r1, 0:T], in_=banks[B_L1][:], func=AF.Ln),
                              [("ps", B_L1)], [("scr", r1)])

                    def s1():
                        S_.op("pe", lambda: nc.tensor.matmul(banks[B_EP][:], lhsT=ones_f[:], rhs=Lacc[:, 0, :],
                                                             start=True, stop=True),
                              ["ones_f", ("Lacc", 0)], [("ps", B_EP)])

                    def s2():
                        S_.op("act", lambda: nc.scalar.activation(out=scr[:, r1, 0:T], in_=scr[:, r1, 0:T], func=AF.Exp, scale=-1.0),
                              [("scr", r1)], [("scr", r1)])
                        S_.op("act", lambda: nc.scalar.activation(out=scr[:, r0, 0:T], in_=banks[B_EP][:], func=AF.Ln),
                              [("ps", B_EP)], [("scr", r0)])
                        S_.op("act", lambda: nc.scalar.activation(out=scr[:, r0, 0:T], in_=scr[:, r0, 0:T], func=AF.Exp, scale=-1.0),
                              [("scr", r0)], [("scr", r0)])

                    def s3():
                        S_.op("dve", lambda: nc.vector.tensor_tensor(out=scr[:, os0, 0:T], in0=scr[:, os0, 0:T], in1=scr[:, r0, 0:T], op=ALU.mult),
                              [("scr", os0), ("scr", r0)], [("scr", os0)])
                        S_.op("dve", lambda: nc.vector.tensor_tensor(out=scr[:, os1, 0:T], in0=scr[:, os1, 0:T], in1=scr[:, r1, 0:T], op=ALU.mult),
                              [("scr", os1), ("scr", r1)], [("scr", os1)])
                        S_.op("dve", lambda: nc.vector.scalar_tensor_tensor(
                            out=scr[:, os0, 0:T], in0=scr[:, os1, 0:T], scalar=small[:, SM_NEGLAM:SM_NEGLAM + 1], in1=scr[:, os0, 0:T],
                            op0=ALU.mult, op1=ALU.add), [("scr", os1), ("scr", os0), "neglam"], [("scr", os0)])

                    def s4():
                        S_.op("act", lambda: nc.scalar.activation(out=sqb[:], in_=scr[:, os0, 0:T], func=AF.Square),
                              [("scr", os0)], ["sqb"])

                    def s5():
                        S_.op("pe", lambda: nc.tensor.matmul(banks[B_EP][:], lhsT=ones[:], rhs=sqb[:], start=True, stop=True),
                              ["ones", "sqb"], [("ps", B_EP)])

                    def s6():
                        S_.op("act", lambda: nc.scalar.activation(out=scr[:, r0, 0:T], in_=banks[B_EP][:], func=AF.Ln,
                                                                  bias=EPS, scale=1.0 / 128), [("ps", B_EP)], [("scr", r0)])
                        S_.op("act", lambda: nc.scalar.activation(out=scr[:, r0, 0:T], in_=scr[:, r0, 0:T], func=AF.Exp, scale=-0.5),
                              [("scr", r0)], [("scr", r0)])

                    def s7():
                        S_.op("dve", lambda: nc.vector.scalar_tensor_tensor(
                            out=onT[:, h, :], in0=scr[:, os0, 0:T], scalar=small[:, SM_SN08:SM_SN08 + 1], in1=scr[:, r0, 0:T],
                            op0=ALU.mult, op1=ALU.mult), [("scr", os0), ("scr", r0), "sn08"], [("onT", h)])

                    return [e1, sL, s1, s2, s3, s4, s5, s6, s7]

                for h in range(4):
                    info = {}

                    def emit_qk(g):
                        kt = g
                        r = kt - 4 * j
                        qlo = 128 * r if r > 0 else 0
                        N = T - qlo
                        sb0 = 4 + (g % 2) * 2
                        info[g] = (qlo, N, sb0)
                        for rep in range(1 + NDUP):
                            for n in range(2):
                                S_.op("pe", lambda n=n: nc.tensor.matmul(
                                    banks[sb0 + n][:, 0:N], lhsT=kT[n * 64:(n + 1) * 64, h, kt * 128:(kt + 1) * 128],
                                    rhs=qT[n * 64:(n + 1) * 64, h, qlo:T], start=True, stop=True),
                                    [("kT", kt)] + QT_ALL[qlo // 128:], [("ps", sb0 + n)])

                    emit_qk(0)
                    if nkt > 1:
                        emit_qk(1)
                    for g in range(nkt):
                        qlo, N, sb0 = info[g]
                        kt = g
                        pis = [(g % 3) * 2 + n for n in range(2)]
                        for n in range(2):
                            pi = pis[n]
                            S_.op("act", lambda n=n, pi=pi: nc.scalar.activation(
                                out=Pb[:, pi, 0:N], in_=banks[sb0 + n][:, 0:N], func=AF.Exp, scale=0.125),
                                [("ps", sb0 + n)], [("P", pi)])
                            if kt >= 4 * j:
                                S_.op("pool", lambda pi=pi: nc.gpsimd.memset(Pb[64:128, pi, 0:64], 0.0), [], [("P", pi)])
                        for _ in range(2 if g == 0 else 1):
                            if pending:
                                pending.pop(0)()
                        if g + 2 < nkt:
                            emit_qk(g + 2)
                        first, last = (kt == 0), (kt == nkt - 1)
                        PK = [("P", pis[0]), ("P", pis[1])]
                        for n in range(2):
                            pi = pis[n]
                            S_.op("pe", lambda n=n, pi=pi: nc.tensor.matmul(
                                banks[n][:, qlo:T], lhsT=Vs[:, kt, h * 128:(h + 1) * 128], rhs=Pb[:, pi, 0:N],
                                start=first, stop=last),
                                [("V", kt)] + PK, [("ps", n)])
                        p0, p1 = pis
                        S_.op("pe", lambda p1=p1: nc.tensor.matmul(
                            banks[B_L1][:, qlo:T], lhsT=ones[:], rhs=Pb[:, p1, 0:N], start=first, stop=last),
                            ["ones"] + PK, [("ps", B_L1)])
                        if g == 0:
                            S_.op("dve", lambda p0=p0: nc.vector.tensor_copy(Lacc[:, 0, :], Pb[:, p0, :]),
                                  [("P", p0)], [("Lacc", 0)])
                        else:
                            S_.op("dve", lambda p0=p0: nc.vector.tensor_tensor(
                                out=Lacc[:, 0, qlo:T], in0=Lacc[:, 0, qlo:T], in1=Pb[:, p0, 0:N], op=ALU.add),
                                [("P", p0), ("Lacc", 0)], [("Lacc", 0)])
                    while pending:
                        pending.pop(0)()
                    ep = make_epilogue(h)
                    ep[0]()
                    pending = ep[1:]
                for dc in range(8):
                    conv_out_part(j, s_co, [dc])
                    if pending:
                        pending.pop(0)()
                while pending:
                    pending.pop(0)()

            for j in range(NST):
                if j == 0:
                    for t in range(4):
                        norm_pre(xt[:, t, :], [("xt", t)], t)
                for t in range(4):
                    norm_T(t, C_GMIX, bk=(t if j > 0 else None))

                for qk, c0 in ((0, C_QN), (1, C_KN)):
                    for tt in range(4):
                        for half in range(2):
                            S_.op("dve", lambda qk=qk, c0=c0, tt=tt, half=half: nc.vector.tensor_tensor(
                                out=tabs[:, 2 * qk, tt, half * 32:(half + 1) * 32], in0=cos_t[:, j * 4 + tt, :],
                                in1=cst[:, c0 + half * 32:c0 + (half + 1) * 32], op=ALU.mult),
                                ["cos_t", "cst"], [("tab", 2 * qk, tt)])
                            S_.op("dve", lambda qk=qk, tt=tt, half=half: nc.vector.tensor_tensor(
                                out=tabs[:, 2 * qk + 1, tt, half * 32:(half + 1) * 32], in0=sin_t[:, j * 4 + tt, :],
                                in1=gsw[:, qk, half * 32:(half + 1) * 32], op=ALU.mult),
                                ["sin_t", ("gsw", qk, 0), ("gsw", qk, 1)], [("tab", 2 * qk + 1, tt)])

                phaseB(j)

                s_co = wload(10)
                if j > 0:
                    for t in range(4):
                        x_load(j, t)
                attention(j, s_co)

                s_ao = wload(11)
                wao = wring[:, s_ao, :].rearrange("p (k c) -> p k c", k=4)
                for dc in range(8):
                    b2 = new_bank()
                    for kc in range(4):
                        S_.op("pe", lambda kc=kc, dc=dc, b2=b2: nc.tensor.matmul(
                            banks[b2][:], lhsT=wao[:, kc, dc * 128:(dc + 1) * 128], rhs=onT[:, kc, :],
                            start=(kc == 0), stop=(kc == 3)), [("w", s_ao), ("onT", kc)], [("ps", b2)])
                    m2 = new_scr()
                    S_.op("dve", lambda dc=dc, b2=b2, m2=m2: nc.vector.tensor_tensor(
                        out=scr[:, m2, 0:T], in0=banks[b2][:], in1=big[:, 8 + dc, :], op=ALU.mult),
                        [("ps", b2), ("big", 8 + dc)], [("scr", m2)])
                    S_.op("dve", lambda dc=dc, m2=m2: nc.vector.tensor_tensor(
                        out=mixT[:, dc, :], in0=scr[:, m2, 0:T], in1=mixT[:, dc, :], op=ALU.add),
                        [("scr", m2), ("mixT", dc)], [("mixT", dc)])


                MIX_ALL = [("mixT", dc) for dc in range(8)]
                s_o = [wload(12), wload(13)]
                for t in range(4):
                    for c in range(2):
                        wv = wring[:, s_o[c], :].rearrange("p (k c) -> p k c", k=8)
                        bk = new_bank()
                        for kc in range(8):
                            S_.op("pe", lambda kc=kc, t=t, bk=bk, wv=wv: nc.tensor.matmul(
                                banks[bk][:], lhsT=mixT[:, kc, t * 128:(t + 1) * 128], rhs=wv[:, kc, :],
                                start=(kc == 0), stop=(kc == 7)), [("w", s_o[c])] + MIX_ALL, [("ps", bk)])
                        S_.op("dve", lambda t=t, bk=bk, c=c: nc.vector.tensor_tensor(
                            out=xt[:, t, c * 512:(c + 1) * 512], in0=banks[bk][:], in1=xt[:, t, c * 512:(c + 1) * 512],
                            op=ALU.add), [("ps", bk), ("xt", t)], [("xt", t)])
                    norm_pre(xt[:, t, :], [("xt", t)], t)
                    if t > 0:
                        norm_T(t - 1, C_GFFN)
                norm_T(3, C_GFFN)

                for jj in range(11):
                    s = wload(14 + jj)
                    wv = wring[:, s, :].rearrange("p (k c) -> p k c", k=8)
                    for c2 in range(2):
                        fi = jj * 2 + c2
                        bg = new_bank()
                        for kc in range(8):
                            S_.op("pe", lambda kc=kc, c2=c2, bg=bg, wv=wv: nc.tensor.matmul(
                                banks[bg][:], lhsT=wv[:, kc, c2 * 128:(c2 + 1) * 128], rhs=hT[:, kc, :],
                                start=(kc == 0), stop=(kc == 7)), [("w", s)] + HT_ALL, [("ps", bg)])
                        bu = new_bank()
                        for kc in range(8):
                            S_.op("pe", lambda kc=kc, c2=c2, bu=bu, wv=wv: nc.tensor.matmul(
                                banks[bu][:], lhsT=wv[:, kc, 256 + c2 * 128:256 + (c2 + 1) * 128], rhs=hT[:, kc, :],
                                start=(kc == 0), stop=(kc == 7)), [("w", s)] + HT_ALL, [("ps", bu)])
                        sg = new_scr()
                        S_.op("act", lambda bg=bg, sg=sg: nc.scalar.activation(out=scr[:, sg, 0:T], in_=banks[bg][:], func=AF.Silu),
                              [("ps", bg)], [("scr", sg)])
                        S_.op("dve", lambda bu=bu, sg=sg, fi=fi: nc.vector.tensor_tensor(
                            out=big[:, fi, :], in0=banks[bu][:], in1=scr[:, sg, 0:T], op=ALU.mult),
                            [("ps", bu), ("scr", sg)], [("big", fi)])

                for i in range(6):
                    s = wload(25 + i)
                    wv = wring[:, s, :].rearrange("p (k c) -> p k c", k=4)
                    for kl in range(4):
                        kc = 4 * i + kl
                        if kc >= NFC:
                            break
                        for t in range(4):
                            for c in range(2):
                                bk = t * 2 + c
                                S_.op("pe", lambda kc=kc, kl=kl, t=t, c=c, bk=bk, wv=wv: nc.tensor.matmul(
                                    banks[bk][:], lhsT=big[:, kc, t * 128:(t + 1) * 128], rhs=wv[:, kl, c * 512:(c + 1) * 512],
                                    start=(kc == 0), stop=(kc == NFC - 1)), [("w", s), ("big", kc)], [("ps", bk)])
                    if j + 1 < NST and 1 <= i <= 4:
                        stage_norm(j + 1, i - 1)
                for t in range(4):
                    for c in range(2):
                        bk = t * 2 + c
                        S_.op("dve", lambda t=t, c=c, bk=bk: nc.vector.tensor_tensor(
                            out=xt[:, t, c * 512:(c + 1) * 512], in0=banks[bk][:], in1=xt[:, t, c * 512:(c + 1) * 512],
                            op=ALU.add), [("ps", bk), ("xt", t)], [("xt", t)])
                    r0 = j * T + t * 128
                    S_.dma("act", [lambda t=t, r0=r0: nc.scalar.dma_start(out=out_d[r0:r0 + 128, :], in_=xt[:, t, :])],
                           [("xt", t)], [("out", j, t)], ("st", t))

        an = Sched(nc, es, needed=None)
        program(an)
        needed = an.analyze()
        em = Sched(nc, es, needed=needed, chans=list(an.chan_val.keys()))
        program(em)
        em.finish([("st", t) for t in range(4)])
    return nc


def _prep_weights(w_in, w_conv_out, w_attn_out, w_o, w_gate_up, w_down):
    blk = np.zeros((NBLK, 128, 4096), dtype=np.float32)

    def put(i, w2d):
        K, C = w2d.shape
        a = w2d.reshape(K // 128, 128, C).transpose(1, 0, 2).reshape(128, -1)
        blk[i, :, :a.shape[1]] = a

    for b in range(10):
        put(b, w_in[:, b * 512:(b + 1) * 512])
    put(10, w_conv_out)
    put(11, w_attn_out)
    for c in range(2):
        put(12 + c, w_o[:, c * 512:(c + 1) * 512])
    for jj in range(11):
        put(14 + jj, np.concatenate([w_gate_up[:, jj * 256:(jj + 1) * 256],
                                     w_gate_up[:, DFF + jj * 256:DFF + (jj + 1) * 256]], axis=1))
    for i in range(6):
        put(25 + i, w_down[i * 512:min((i + 1) * 512, DFF), :])
    return blk


def _prep_consts(g_mix, g_ffn, b_gate, conv_w, q_norm, k_norm, lq1, lk1, lq2, lk2, sub_norm):
    c = np.zeros((128, CW), dtype=np.float32)
    c[:, C_GMIX:C_GMIX + 8] = g_mix.reshape(8, 128).T
    c[:, C_GFFN:C_GFFN + 8] = g_ffn.reshape(8, 128).T
    c[:, C_BG:C_BG + 16] = b_gate.reshape(16, 128).T
    c[:, C_CW:C_CW + 12] = conv_w.reshape(3, 4, 128).transpose(2, 1, 0).reshape(128, 12)
    c[:, C_QN:C_QN + 64] = q_norm[None, :]
    c[:, C_KN:C_KN + 64] = k_norm[None, :]
    c[:, C_LAM:C_LAM + 256] = np.concatenate([lq1, lq2, lk1, lk2])[None, :]
    c[:, C_SN] = sub_norm
    inv = (1.0 / (np.float32(10000.0) ** (np.arange(0, 64, 2, dtype=np.float32) / np.float32(64)))).astype(np.float32)
    c[:, C_INV:C_INV + 32] = inv[None, :]
    return c


_NC_CACHE = {}


def kernel(x, g_mix, w_in, b_gate, conv_w, q_norm, k_norm, lambda_q1, lambda_k1, lambda_q2, lambda_k2,
           sub_norm, w_conv_out, w_attn_out, w_o, g_ffn, w_gate_up, w_down):
    f = lambda a: np.ascontiguousarray(np.asarray(a, dtype=np.float32))
    x = f(x)
    wblk = _prep_weights(f(w_in)[0], f(w_conv_out)[0], f(w_attn_out)[0], f(w_o)[0], f(w_gate_up)[0], f(w_down)[0])
    cst = _prep_consts(f(g_mix)[0], f(g_ffn)[0], f(b_gate)[0], f(conv_w)[0], f(q_norm)[0], f(k_norm)[0],
                       f(lambda_q1)[0], f(lambda_k1)[0], f(lambda_q2)[0], f(lambda_k2)[0], f(sub_norm)[0])
    if "nc" not in _NC_CACHE:
        _NC_CACHE["nc"] = build_program()
    nc = _NC_CACHE["nc"]
    in_maps = [{"x": x[c], "wblk": wblk, "cst": cst} for c in range(NCORES)]
    res = run_bass_kernel_spmd(nc, in_maps, core_ids=list(range(NCORES)))
    out = np.stack([np.asarray(res.results[c]["out"], dtype=np.float32) for c in range(NCORES)], axis=0)
    return out
```

```python
import contextlib
import math

import numpy as np
import concourse.bass as bass
import concourse.mybir as mybir
from concourse.bass_utils import run_bass_kernel_spmd

F32 = mybir.dt.float32
BF16 = mybir.dt.bfloat16
I32 = mybir.dt.int32
AF = mybir.ActivationFunctionType
ALU = mybir.AluOpType
AX = mybir.AxisListType

D = 1024
S = 4096
NCORES = 8
T = 512
NST = S // T
DFF = 2816
NFC = DFF // 128
EPS = 1e-6
LAM_INIT = 0.8 - 0.6 * math.exp(0.0)
NBLK = 31
NSLOT = 3
LOOKAHEAD = 2
NDUP = 0

C_GMIX, C_GFFN, C_BG, C_CW, C_QN, C_KN, C_LAM, C_SN, C_INV, C_END = 0, 8, 16, 32, 44, 108, 172, 428, 429, 461
CW = 464


class Sched:
    def __init__(self, nc, es, needed=None, chans=()):
        self.nc = nc
        self.es = es
        self.emit_mode = needed is not None
        self.needed = needed
        self.ops = []
        self.deps_all = []
        self.lastw = {}
        self.readers = {}
        self.engs = {"pe": nc.tensor, "act": nc.scalar, "dve": nc.vector, "pool": nc.gpsimd, "sp": nc.sync}
        self.chan_val = {}
        if self.emit_mode:
            self.sems = {e: es.enter_context(nc.semaphore("sem_" + e)) for e in ("pe", "act", "dve", "pool")}
            self.csems = {c: es.enter_context(nc.semaphore("ch_" + "_".join(str(k) for k in (c if isinstance(c, tuple) else (c,)))))
                          for c in chans}
            self.counts = {e: 0 for e in self.sems}
            self.seen = {e: {} for e in self.engs}

    def _deps(self, reads, writes):
        deps = set()
        for r in reads:
            w = self.lastw.get(r)
            if w is not None:
                deps.add(w)
        for w_ in writes:
            w = self.lastw.get(w_)
            if w is not None:
                deps.add(w)
            for rd in self.readers.get(w_, {}).values():
                deps.add(rd)
        return deps

    def _commit(self, idx, skey, reads, writes):
        for r in reads:
            self.readers.setdefault(r, {})[skey] = idx
        for w in writes:
            self.lastw[w] = idx
            self.readers[w] = {}

    def _waits(self, eng_name, kind, deps):
        eng = self.engs[eng_name]
        sn = self.seen[eng_name]
        waits = {}
        for d in deps:
            p_eng, p_kind, p_chan, p_v = self.ops[d]
            if p_kind == "c":
                if p_eng == "pe" and eng_name == "pe" and kind == "c":
                    continue
                key = ("e", p_eng)
            else:
                key = ("c", p_chan)
            if p_v > waits.get(key, -1):
                waits[key] = p_v
        for key, v in waits.items():
            if sn.get(key, -1) >= v:
                continue
            sn[key] = v
            sem = self.sems[key[1]] if key[0] == "e" else self.csems[key[1]]
            eng.wait_ge(sem, v)

    def op(self, eng, fn, reads=(), writes=()):
        deps = self._deps(reads, writes)
        idx = len(self.ops)
        if self.emit_mode:
            self._waits(eng, "c", deps)
            ins = fn()
            cnt = None
            if self.needed[idx]:
                self.counts[eng] += 1
                cnt = self.counts[eng]
                ins.then_inc(self.sems[eng], 1)
            self.ops.append((eng, "c", None, cnt))
        else:
            self.ops.append((eng, "c", None, None))
            self.deps_all.append(deps)
        self._commit(idx, eng, reads, writes)
        return idx

    def dma(self, queue, fns, reads, writes, chan):
        deps = self._deps(reads, writes)
        idx = len(self.ops)
        self.chan_val[chan] = self.chan_val.get(chan, 0) + 16 * len(fns)
        if self.emit_mode:
            self._waits(queue, "d", deps)
            for f in fns:
                f().then_inc(self.csems[chan], 16)
        else:
            self.deps_all.append(deps)
        self.ops.append((queue, "d", chan, self.chan_val[chan]))
        self._commit(idx, ("ch", chan), reads, writes)
        return idx

    def analyze(self):
        needed = [False] * len(self.ops)
        for i, deps in enumerate(self.deps_all):
            o_eng, o_kind, _, _ = self.ops[i]
            for d in deps:
                p_eng, p_kind, _, _ = self.ops[d]
                if p_kind == "c" and not (p_eng == "pe" and o_eng == "pe" and o_kind == "c"):
                    needed[d] = True
        return needed

    def finish(self, final_chans):
        for c in final_chans:
            self.nc.gpsimd.wait_ge(self.csems[c], self.chan_val[c])


def build_program():
    nc = bass.Bass("TRN2", target_bir_lowering=False)
    x_d = nc.dram_tensor("x", [S, D], F32, kind="ExternalInput").ap()
    wblk_d = nc.dram_tensor("wblk", [NBLK, 128, 4096], F32, kind="ExternalInput").ap()
    cst_d = nc.dram_tensor("cst", [128, CW], F32, kind="ExternalInput").ap()
    out_d = nc.dram_tensor("out", [S, D], F32, kind="ExternalOutput").ap()
    wsc_d = nc.dram_tensor("wsc", [NBLK, 128, 4096], BF16, kind="Internal").ap()

    with contextlib.ExitStack() as es:

        def sb(name, shape, dt):
            return es.enter_context(nc.sbuf_tensor(name, shape, dt))

        banks = [es.enter_context(nc.psum_tensor("bank%d" % i, [128, 512], F32)) for i in range(8)]
        kT = sb("kT", [128, 4, S], BF16)
        Vs = sb("Vs", [128, S // 128, 512], BF16)
        xt = sb("xt", [128, 4, D], F32)
        xn = sb("xn", [128, 4, D], BF16)
        hT = sb("hT", [128, 8, T], BF16)
        mixT = sb("mixT", [128, 8, T], BF16)
        bT = mixT[:, 0:4, :]
        cT = mixT[:, 4:8, :]
        halo = sb("halo", [128, 4, 2], F32)
        byT = sb("byT", [128, 4, T], BF16)
        NYQ = 3
        yq = sb("yq", [128, NYQ, 512], BF16)
        qT = sb("qT", [128, 4, T], BF16)
        big = sb("big", [128, NFC, T], BF16)
        NP = 6
        Pb = sb("Pb", [128, NP, T], BF16)
        onT = sb("onT", [128, 4, T], BF16)
        sqb = sb("sqb", [128, T], BF16)
        NSCR = 7
        SCW = 516
        scr = sb("scr", [128, NSCR, SCW], F32)
        Lacc = sb("Lacc", [128, 2, T], F32)
        ones_f = sb("ones_f", [128, 128], F32)
        cos_t = sb("cos_t", [128, 32, 32], F32)
        sin_t = sb("sin_t", [128, 32, 32], F32)
        wring = sb("wring", [128, NSLOT, 4096], BF16)
        cst = sb("cst_sb", [128, CW], F32)
        ident = sb("ident", [128, 128], BF16)
        ones = sb("ones", [128, 128], BF16)
        tabs = sb("tabs", [128, 4, 4, 64], F32)
        gsw = sb("gsw", [128, 2, 64], F32)
        small = sb("small", [128, 64], F32)
        ss8 = sb("ss8", [128, 2, 8], F32)
        sd8 = sb("sd8", [128, 2, 8], F32)
        rs8 = sb("rs8", [128, 2, 8], F32)
        bigf = big[:].rearrange("p a b -> p (a b)").bitcast(F32)
        setup_f = bigf[:, 0:1024].rearrange("p (a b) -> p a b", b=32)
        setup_i = bigf[:, 1024:2048].bitcast(I32).rearrange("p (a b) -> p a b", b=32)
        KSF = [("big", i) for i in range(0, 4)]
        KSI = [("big", i) for i in range(4, 8)]

        SM_NEGLAM, SM_SN08, SM_E, SM_SS, SM_SD, SM_RS = 0, 1, 2, 8, 16, 24

        def program(S_):
            scr_i = [0]

            def new_scr():
                i = scr_i[0] % NSCR
                scr_i[0] += 1
                return i

            bank_i = [0]

            def new_bank(lo=0, hi=8):
                n = hi - lo
                i = lo + (bank_i[0] % n)
                bank_i[0] += 1
                return i

            def bankbf(i):
                return banks[i][:].bitcast(BF16)

            S_.dma("sp", [lambda: nc.sync.dma_start(out=cst[:], in_=cst_d)], [], ["cst"], "cst")

            def x_load(j, t):
                r0 = j * T + t * 128
                if j == 0:
                    S_.dma("act", [lambda: nc.scalar.dma_start(out=xt[:, t, :], in_=x_d[r0:r0 + 128, :])],
                           [], [("xt", t)], ("xl0", t))
                else:
                    S_.dma("sp", [lambda: nc.sync.dma_start(out=xt[:, t, :], in_=x_d[r0:r0 + 128, :])],
                           [], [("xt", t)], ("xl", t))

            xs = Lacc[:].rearrange("p a b -> p (a b)")
            XS_KEYS = [("Lacc", 0), ("Lacc", 1)]

            def stage_norm(j, t):
                r0 = j * T + t * 128
                S_.dma("act", [lambda: nc.scalar.dma_start(out=xs, in_=x_d[r0:r0 + 128, :])], [], XS_KEYS, "xs")
                norm_pre(xs, XS_KEYS, t)

            for t in range(4):
                x_load(0, t)

            S_.op("pool", lambda: nc.gpsimd.memset(setup_f[:, 0:4, :], 0.0), [], KSF)
            idf = setup_f[:, 0:4, :].rearrange("p a b -> p (a b)")
            S_.op("pool", lambda: nc.gpsimd.affine_select(idf, idf, [[-1, 128]], ALU.not_equal, 1.0, base=0,
                                                          channel_multiplier=1), [], KSF)
            S_.op("dve", lambda: nc.vector.tensor_copy(ident[:], idf), KSF, ["ident"])
            S_.op("dve", lambda: nc.vector.memset(ones[:], 1.0), [], ["ones"])
            S_.op("dve", lambda: nc.vector.memset(halo[:], 0.0), [], [("halo", c) for c in range(4)])
            S_.op("dve", lambda: nc.vector.memset(ones_f[:], 1.0), [], ["ones_f"])

            pos_i = setup_i[:, 0, :]
            S_.op("pool", lambda: nc.gpsimd.iota(pos_i, [[128, 32]], base=0, channel_multiplier=1), KSF, KSI)
            pos_f = small[:, 32:64]
            S_.op("dve", lambda: nc.vector.tensor_copy(pos_f, pos_i), KSI, ["pos_f"])
            inv = cst[:, C_INV:C_INV + 32]
            ang = setup_f
            TWO_PI = 2.0 * math.pi

            def trig(dst, dkey, shift):
                S_.op("dve", lambda: nc.vector.tensor_tensor(
                    out=ang, in0=pos_f.unsqueeze(2).to_broadcast([128, 32, 32]),
                    in1=inv.unsqueeze(1).to_broadcast([128, 32, 32]), op=ALU.mult),
                    ["pos_f", "cst"], KSF)
                if shift != 0.0:
                    S_.op("dve", lambda: nc.vector.tensor_scalar(out=ang, in0=ang, scalar1=shift, scalar2=None,
                                                                 op0=ALU.add), KSF, KSF)
                S_.op("dve", lambda: nc.vector.tensor_scalar(out=dst[:], in0=ang, scalar1=1.0 / TWO_PI, scalar2=None,
                                                             op0=ALU.mult), KSF, [dkey])
                S_.op("dve", lambda: nc.vector.tensor_copy(setup_i, dst[:]), [dkey], KSI)
                S_.op("dve", lambda: nc.vector.tensor_copy(dst[:], setup_i), KSI, [dkey])
                S_.op("dve", lambda: nc.vector.scalar_tensor_tensor(out=dst[:], in0=dst[:], scalar=-TWO_PI, in1=ang,
                                                                    op0=ALU.mult, op1=ALU.add),
                      [dkey] + KSF, [dkey])
                S_.op("dve", lambda: nc.vector.tensor_scalar(out=dst[:], in0=dst[:], scalar1=3.1415925, scalar2=-3.1415925,
                                                             op0=ALU.min, op1=ALU.max), [dkey], [dkey])
                S_.op("act", lambda: nc.scalar.activation(out=dst[:], in_=dst[:], func=AF.Sin), [dkey], [dkey])

            trig(sin_t, "sin_t", 0.0)
            trig(cos_t, "cos_t", math.pi / 2.0)

            lamv = cst[:, C_LAM:C_LAM + 256].rearrange("p (a d) -> p a d", a=4)
            lprod = scr[:, 4, 0:128].rearrange("p (a d) -> p a d", a=2)
            S_.op("dve", lambda: nc.vector.tensor_tensor(out=lprod, in0=lamv[:, 0:2, :], in1=lamv[:, 2:4, :], op=ALU.mult),
                  ["cst"], [("scr", 4)])
            S_.op("dve", lambda: nc.vector.tensor_reduce(out=small[:, SM_E:SM_E + 2], in_=lprod, axis=AX.X, op=ALU.add),
                  [("scr", 4)], ["sm_e"])
            S_.op("act", lambda: nc.scalar.activation(out=small[:, SM_E + 2:SM_E + 4], in_=small[:, SM_E:SM_E + 2], func=AF.Exp),
                  ["sm_e"], ["sm_e2"])
            S_.op("dve", lambda: nc.vector.tensor_tensor(out=small[:, SM_E + 4:SM_E + 5], in0=small[:, SM_E + 3:SM_E + 4],
                                                         in1=small[:, SM_E + 2:SM_E + 3], op=ALU.subtract),
                  ["sm_e2"], ["sm_e3"])
            S_.op("dve", lambda: nc.vector.tensor_scalar(out=small[:, SM_NEGLAM:SM_NEGLAM + 1], in0=small[:, SM_E + 4:SM_E + 5],
                                                         scalar1=-LAM_INIT, scalar2=None, op0=ALU.add),
                  ["sm_e3"], ["neglam"])
            S_.op("dve", lambda: nc.vector.tensor_scalar(out=small[:, SM_SN08:SM_SN08 + 1], in0=cst[:, C_SN:C_SN + 1],
                                                         scalar1=1.0 - LAM_INIT, scalar2=None, op0=ALU.mult),
                  ["cst"], ["sn08"])
            for qk, c0 in ((0, C_QN), (1, C_KN)):
                S_.op("dve", lambda qk=qk, c0=c0: nc.vector.tensor_scalar(
                    out=gsw[:, qk, 0:32], in0=cst[:, c0 + 32:c0 + 64], scalar1=-1.0, scalar2=None, op0=ALU.mult),
                    ["cst"], [("gsw", qk, 0)])
                S_.op("dve", lambda qk=qk, c0=c0: nc.vector.tensor_copy(gsw[:, qk, 32:64], cst[:, c0:c0 + 32]),
                      ["cst"], [("gsw", qk, 1)])

            WORDER = [3, 0, 4, 1, 5, 2, 6, 7, 8, 9] + list(range(10, NBLK))
            wseq = [(jj, b) for jj in range(NST) for b in WORDER]
            wptr = [0]
            wissued = [0]
            NSTG = 3
            stg_i = [0]

            def stg_view(hb):
                return Vs[:, 8 + 8 * hb:16 + 8 * hb, :].rearrange("p a b -> p (a b)").bitcast(F32)

            def stg_keys(hb):
                return [("V", kk) for kk in range(8 + 8 * hb, 16 + 8 * hb)]

            def issue(k):
                jj, b = wseq[k]
                sl = k % NSLOT
                if jj > 0:
                    S_.dma("sp", [lambda: nc.sync.dma_start(out=wring[:, sl, :], in_=wsc_d[b])],
                           [("wsc", b)], [("w", sl)], ("wr", sl))
                    return
                for half in range(2):
                    hb = stg_i[0] % NSTG
                    stg_i[0] += 1
                    sv = stg_view(hb)
                    S_.dma("sp", [lambda half=half, sv=sv: nc.sync.dma_start(out=sv, in_=wblk_d[b][:, half * 2048:(half + 1) * 2048])],
                           [], stg_keys(hb), ("stg", hb))
                    dst = wring[:, sl, half * 2048:(half + 1) * 2048]
                    if half == 0:
                        S_.op("act", lambda dst=dst, sv=sv: nc.scalar.copy(out=dst, in_=sv), stg_keys(hb), [("w", sl)])
                    else:
                        S_.op("dve", lambda dst=dst, sv=sv: nc.vector.tensor_copy(dst, sv), stg_keys(hb), [("w", sl)])
                S_.dma("pool", [lambda: nc.gpsimd.dma_start(out=wsc_d[b], in_=wring[:, sl, :])],
                       [("w", sl)], [("wsc", b)], ("cv", b))

            def wload(b):
                k = wptr[0]
                assert wseq[k][1] == b, (wseq[k], b)
                if wseq[k][0] == 0:
                    depth = 1 if (k >= 1 and wseq[k - 1][1] == 12) else 2
                    lim = min(k + depth, NBLK - 1)
                else:
                    lim = k
                while wissued[0] <= lim:
                    issue(wissued[0])
                    wissued[0] += 1
                wptr[0] += 1
                return k % NSLOT

            def norm_pre(src, src_keys, t):
                S_.op("dve", lambda: nc.vector.memset(small[:, SM_SS + t:SM_SS + t + 1], 0.0), [], [("ss", t)])
                S_.op("act", lambda: nc.scalar.activation(out=xn[:, t, :], in_=src, func=AF.Square,
                                                          accum_out=small[:, SM_SS + t:SM_SS + t + 1]),
                      list(src_keys) + [("ss", t)], [("xn", t), ("ss", t)])
                S_.op("act", lambda: nc.scalar.activation(out=small[:, SM_SD + t:SM_SD + t + 1],
                                                          in_=small[:, SM_SS + t:SM_SS + t + 1], func=AF.Ln,
                                                          bias=EPS, scale=1.0 / D),
                      [("ss", t)], [("sd", t)])
                S_.op("act", lambda: nc.scalar.activation(out=small[:, SM_RS + t:SM_RS + t + 1],
                                                          in_=small[:, SM_SD + t:SM_SD + t + 1], func=AF.Exp, scale=-0.5),
                      [("sd", t)], [("rs", t)])
                S_.op("dve", lambda: nc.vector.tensor_scalar(out=xn[:, t, :], in0=src,
                                                             scalar1=small[:, SM_RS + t:SM_RS + t + 1], scalar2=None,
                                                             op0=ALU.mult),
                      list(src_keys) + [("rs", t)], [("xn", t)])

            def norm_T(t, gcol0, bk=None):
                if bk is None:
                    bk = new_bank()
                bv = bankbf(bk).rearrange("p (c k) -> p c k", c=8)
                for kc in range(8):
                    S_.op("pe", lambda kc=kc: nc.tensor.transpose(bv[:, kc, :], xn[:, t, kc * 128:(kc + 1) * 128], ident[:]),
                          [("xn", t), "ident"], [("ps", bk)])
                S_.op("dve", lambda: nc.vector.tensor_tensor(
                    out=hT[:, :, t * 128:(t + 1) * 128], in0=bv,
                    in1=cst[:, gcol0:gcol0 + 8].unsqueeze(2).to_broadcast([128, 8, 128]), op=ALU.mult),
                    [("ps", bk), "cst"], [("hT", t)])

            HT_ALL = [("hT", t) for t in range(4)]
            QT_ALL = [("qT", t) for t in range(4)]

            def rstd_small(src, dst, tmp, scale, rk, wk, tk):
                S_.op("act", lambda: nc.scalar.activation(out=tmp, in_=src, func=AF.Ln, bias=EPS, scale=scale), [rk], [tk])
                S_.op("act", lambda: nc.scalar.activation(out=dst, in_=tmp, func=AF.Exp, scale=-0.5), [tk], [wk])

            def fm_block(j, b):
                s = wload(b)
                wv = wring[:, s, :].rearrange("p (k c) -> p k c", k=8)
                for c4 in range(4):
                    bk = new_bank()
                    for kc in range(8):
                        S_.op("pe", lambda kc=kc: nc.tensor.matmul(
                            banks[bk][:], lhsT=wv[:, kc, c4 * 128:(c4 + 1) * 128], rhs=hT[:, kc, :],
                            start=(kc == 0), stop=(kc == 7)),
                            [("w", s)] + HT_ALL, [("ps", bk)])
                    if b == 0:
                        S_.op("act", lambda: nc.scalar.copy(out=bT[:, c4, :], in_=banks[bk][:]),
                              [("ps", bk)], [("mixT", c4)])
                    elif b == 1:
                        S_.op("act", lambda: nc.scalar.copy(out=cT[:, c4, :], in_=banks[bk][:]),
                              [("ps", bk)], [("mixT", 4 + c4)])
                    elif b == 2:
                        u = new_scr()
                        S_.op("dve", lambda: nc.vector.tensor_copy(scr[:, u, 0:2], halo[:, c4, :]),
                              [("halo", c4)], [("scr", u)])
                        S_.op("dve", lambda: nc.vector.tensor_tensor(
                            out=scr[:, u, 2:T + 2], in0=banks[bk][:], in1=cT[:, c4, :], op=ALU.mult),
                            [("ps", bk), ("mixT", 4 + c4)], [("scr", u)])
                        S_.op("dve", lambda: nc.vector.tensor_copy(halo[:, c4, :], scr[:, u, T:T + 2]),
                              [("scr", u)], [("halo", c4)])
                        y = new_scr()
                        cw = C_CW + c4 * 3
                        S_.op("dve", lambda: nc.vector.tensor_scalar(
                            out=scr[:, y, 0:T], in0=scr[:, u, 2:T + 2], scalar1=cst[:, cw + 2:cw + 3],
                            scalar2=None, op0=ALU.mult), [("scr", u), "cst"], [("scr", y)])
                        S_.op("dve", lambda: nc.vector.scalar_tensor_tensor(
                            out=scr[:, y, 0:T], in0=scr[:, u, 1:T + 1], scalar=cst[:, cw + 1:cw + 2],
                            in1=scr[:, y, 0:T], op0=ALU.mult, op1=ALU.add), [("scr", u), "cst", ("scr", y)], [("scr", y)])
                        S_.op("dve", lambda: nc.vector.scalar_tensor_tensor(
                            out=scr[:, y, 0:T], in0=scr[:, u, 0:T], scalar=cst[:, cw:cw + 1],
                            in1=scr[:, y, 0:T], op0=ALU.mult, op1=ALU.add), [("scr", u), "cst", ("scr", y)], [("scr", y)])
                        S_.op("dve", lambda: nc.vector.tensor_tensor(
                            out=byT[:, c4, :], in0=scr[:, y, 0:T], in1=bT[:, c4, :], op=ALU.mult),
                            [("scr", y), ("mixT", c4)], [("byT", c4)])
                    else:
                        gi = (b - 6) * 4 + c4
                        S_.op("act", lambda: nc.scalar.activation(
                            out=big[:, gi, :], in_=banks[bk][:], func=AF.Sigmoid,
                            bias=cst[:, C_BG + gi:C_BG + gi + 1], scale=1.0),
                            [("ps", bk), "cst"], [("big", gi)])

            def tm_matmuls(j, b, s, t):
                wv = wring[:, s, :].rearrange("p (k c) -> p k c", k=8)
                bk = new_bank()
                for kc in range(8):
                    S_.op("pe", lambda kc=kc: nc.tensor.matmul(
                        banks[bk][:], lhsT=hT[:, kc, t * 128:(t + 1) * 128], rhs=wv[:, kc, :],
                        start=(kc == 0), stop=(kc == 7)),
                        [("w", s), ("hT", t)], [("ps", bk)])
                return bk

            def qk_chain_stages(j, qk, t, bk):
                yi = (qk * 4 + t) % NYQ
                z, t2 = new_scr(), new_scr()
                z3 = scr[:, z, 0:512].rearrange("p (g d) -> p g d", g=8)
                t23 = scr[:, t2, 0:512].rearrange("p (g d) -> p g d", g=8)
                pb = (qk * 4 + t) % 2

                def sA():
                    S_.op("act", lambda: nc.scalar.copy(out=scr[:, z, 0:512], in_=banks[bk][:]), [("ps", bk)], [("scr", z)])
                    S_.op("dve", lambda: nc.vector.tensor_tensor(out=scr[:, t2, 0:512], in0=scr[:, z, 0:512], in1=scr[:, z, 0:512],
                                                                 op=ALU.mult), [("scr", z)], [("scr", t2)])
                    S_.op("dve", lambda: nc.vector.tensor_reduce(out=ss8[:, pb, :], in_=t23, axis=AX.X, op=ALU.add),
                          [("scr", t2)], [("ss8", pb)])
                    rstd_small(ss8[:, pb, :], rs8[:, pb, :], sd8[:, pb, :], 1.0 / 64, ("ss8", pb), ("rs8", pb), ("sd8", pb))

                def sB():
                    S_.op("pool", lambda: nc.gpsimd.tensor_tensor(
                        out=t23[:, :, 0:32], in0=z3[:, :, 32:64],
                        in1=tabs[:, 2 * qk + 1, t, 0:32].unsqueeze(1).to_broadcast([128, 8, 32]), op=ALU.mult),
                        [("scr", z), ("tab", 2 * qk + 1, t)], [("scr", t2)])
                    S_.op("pool", lambda: nc.gpsimd.tensor_tensor(
                        out=t23[:, :, 32:64], in0=z3[:, :, 0:32],
                        in1=tabs[:, 2 * qk + 1, t, 32:64].unsqueeze(1).to_broadcast([128, 8, 32]), op=ALU.mult),
                        [("scr", z), ("tab", 2 * qk + 1, t)], [("scr", t2)])
                    S_.op("pool", lambda: nc.gpsimd.tensor_tensor(
                        out=z3, in0=z3, in1=tabs[:, 2 * qk, t, :].unsqueeze(1).to_broadcast([128, 8, 64]), op=ALU.mult),
                        [("scr", z), ("tab", 2 * qk, t)], [("scr", z)])
                    S_.op("pool", lambda: nc.gpsimd.tensor_tensor(
                        out=scr[:, z, 0:512], in0=scr[:, z, 0:512], in1=scr[:, t2, 0:512], op=ALU.add),
                        [("scr", z), ("scr", t2)], [("scr", z)])

                def sC():
                    S_.op("dve", lambda: nc.vector.tensor_tensor(
                        out=yq[:, yi, :].rearrange("p (g d) -> p g d", g=8), in0=z3,
                        in1=rs8[:, pb, :].unsqueeze(2).to_broadcast([128, 8, 64]), op=ALU.mult),
                        [("scr", z), ("rs8", pb)], [("yq", yi)])

                return {"qk": qk, "t": t, "yi": yi, "stages": [sA, sB, sC], "age": 0}

            def qk_transpose(j, qk, t, yi):
                bk2 = new_bank()
                bv = bankbf(bk2)[:, 0:512].rearrange("p (c k) -> p c k", c=4)
                for h in range(4):
                    S_.op("pe", lambda h=h: nc.tensor.transpose(
                        bv[:, h, :], yq[:, yi, h * 128:(h + 1) * 128], ident[:]),
                        [("yq", yi), "ident"], [("ps", bk2)])
                if qk == 0:
                    S_.op("act", lambda: nc.scalar.copy(out=qT[:, :, t * 128:(t + 1) * 128], in_=bv),
                          [("ps", bk2)], [("qT", t)])
                else:
                    c0 = j * T + t * 128
                    S_.op("act", lambda: nc.scalar.copy(out=kT[:, :, c0:c0 + 128], in_=bv),
                          [("ps", bk2)], [("kT", j * 4 + t)])

            def phaseB(j):
                pipe = []
                pend = []

                def advance():
                    while pend and pend[0]["age"] >= 2:
                        c = pend.pop(0)
                        qk_transpose(j, c["qk"], c["t"], c["yi"])
                    for c in pend:
                        c["age"] += 1
                    for c in list(pipe):
                        if len(c["stages"]) == 1:
                            for p_ in [p_ for p_ in pend if p_["yi"] == c["yi"]]:
                                pend.remove(p_)
                                qk_transpose(j, p_["qk"], p_["t"], p_["yi"])
                        c["stages"].pop(0)()
                        if not c["stages"]:
                            pipe.remove(c)
                            pend.append(c)

                def tm_block(b, qk):
                    sl = wload(b)
                    for t in range(4):
                        bk = tm_matmuls(j, b, sl, t)
                        if qk is None:
                            S_.op("act", lambda t=t, bk=bk: nc.scalar.copy(out=Vs[:, j * 4 + t, :], in_=banks[bk][:]),
                                  [("ps", bk)], [("V", j * 4 + t)])
                        else:
                            pipe.append(qk_chain_stages(j, qk, t, bk))
                        advance()

                def fm(b):
                    fm_block(j, b)
                    advance()

                tm_block(3, 0)
                fm(0)
                tm_block(4, 1)
                fm(1)
                tm_block(5, None)
                for b in (2, 6, 7, 8, 9):
                    fm(b)
                while pipe or pend:
                    advance()

            def conv_out_part(j, s_co, dcs):
                wco = wring[:, s_co, :].rearrange("p (k c) -> p k c", k=4)
                for dc in dcs:
                    b1 = new_bank(4, 8)
                    for kc in range(4):
                        S_.op("pe", lambda kc=kc: nc.tensor.matmul(
                            banks[b1][:], lhsT=wco[:, kc, dc * 128:(dc + 1) * 128], rhs=byT[:, kc, :],
                            start=(kc == 0), stop=(kc == 3)), [("w", s_co), ("byT", kc)], [("ps", b1)])
                    S_.op("dve", lambda: nc.vector.tensor_tensor(
                        out=mixT[:, dc, :], in0=banks[b1][:], in1=big[:, dc, :], op=ALU.mult),
                        [("ps", b1), ("big", dc)], [("mixT", dc)])

            def attention(j, s_co):
                nkt = 4 * j + 4
                pending = []
                B_L1, B_EP = 2, 3

                def make_epilogue(h):
                    os0, os1, r0, r1 = new_scr(), new_scr(), new_scr(), new_scr()

                    def e1():
                        S_.op("dve", lambda: nc.vector.tensor_copy(scr[:, os0, 0:T], banks[0][:]), [("ps", 0)], [("scr", os0)])
                        S_.op("dve", lambda: nc.vector.tensor_copy(scr[:, os1, 0:T], banks[1][:]), [("ps", 1)], [("scr", os1)])

                    def sL():
                        S_.op("act", lambda: nc.scalar.activation(out=scr[:, r1, 0:T], in_=banks[B_L1][:], func=AF.Ln),
                              [("ps", B_L1)], [("scr", r1)])

                    def s1():
                        S_.op("pe", lambda: nc.tensor.matmul(banks[B_EP][:], lhsT=ones_f[:], rhs=Lacc[:, 0, :],
                                                             start=True, stop=True),
                              ["ones_f", ("Lacc", 0)], [("ps", B_EP)])

                    def s2():
                        S_.op("act", lambda: nc.scalar.activation(out=scr[:, r1, 0:T], in_=scr[:, r1, 0:T], func=AF.Exp, scale=-1.0),
                              [("scr", r1)], [("scr", r1)])
                        S_.op("act", lambda: nc.scalar.activation(out=scr[:, r0, 0:T], in_=banks[B_EP][:], func=AF.Ln),
                              [("ps", B_EP)], [("scr", r0)])
                        S_.op("act", lambda: nc.scalar.activation(out=scr[:, r0, 0:T], in_=scr[:, r0, 0:T], func=AF.Exp, scale=-1.0),
                              [("scr", r0)], [("scr", r0)])

                    def s3():
                        S_.op("dve", lambda: nc.vector.tensor_tensor(out=scr[:, os0, 0:T], in0=scr[:, os0, 0:T], in1=scr[:, r0, 0:T], op=ALU.mult),
                              [("scr", os0), ("scr", r0)], [("scr", os0)])
                        S_.op("dve", lambda: nc.vector.tensor_tensor(out=scr[:, os1, 0:T], in0=scr[:, os1, 0:T], in1=scr[:, r1, 0:T], op=ALU.mult),
                              [("scr", os1), ("scr", r1)], [("scr", os1)])
                        S_.op("dve", lambda: nc.vector.scalar_tensor_tensor(
                            out=scr[:, os0, 0:T], in0=scr[:, os1, 0:T], scalar=small[:, SM_NEGLAM:SM_NEGLAM + 1], in1=scr[:, os0, 0:T],
                            op0=ALU.mult, op1=ALU.add), [("scr", os1), ("scr", os0), "neglam"], [("scr", os0)])

                    def s4():
                        S_.op("act", lambda: nc.scalar.activation(out=sqb[:], in_=scr[:, os0, 0:T], func=AF.Square),
                              [("scr", os0)], ["sqb"])

                    def s5():
                        S_.op("pe", lambda: nc.tensor.matmul(banks[B_EP][:], lhsT=ones[:], rhs=sqb[:], start=True, stop=True),
                              ["ones", "sqb"], [("ps", B_EP)])

                    def s6():
                        S_.op("act", lambda: nc.scalar.activation(out=scr[:, r0, 0:T], in_=banks[B_EP][:], func=AF.Ln,
                                                                  bias=EPS, scale=1.0 / 128), [("ps", B_EP)], [("scr", r0)])
                        S_.op("act", lambda: nc.scalar.activation(out=scr[:, r0, 0:T], in_=scr[:, r0, 0:T], func=AF.Exp, scale=-0.5),
                              [("scr", r0)], [("scr", r0)])

                    def s7():
                        S_.op("dve", lambda: nc.vector.scalar_tensor_tensor(
                            out=onT[:, h, :], in0=scr[:, os0, 0:T], scalar=small[:, SM_SN08:SM_SN08 + 1], in1=scr[:, r0, 0:T],
                            op0=ALU.mult, op1=ALU.mult), [("scr", os0), ("scr", r0), "sn08"], [("onT", h)])

                    return [e1, sL, s1, s2, s3, s4, s5, s6, s7]

                for h in range(4):
                    info = {}

                    def emit_qk(g):
                        kt = g
                        r = kt - 4 * j
                        qlo = 128 * r if r > 0 else 0
                        N = T - qlo
                        sb0 = 4 + (g % 2) * 2
                        info[g] = (qlo, N, sb0)
                        for rep in range(1 + NDUP):
                            for n in range(2):
                                S_.op("pe", lambda n=n: nc.tensor.matmul(
                                    banks[sb0 + n][:, 0:N], lhsT=kT[n * 64:(n + 1) * 64, h, kt * 128:(kt + 1) * 128],
                                    rhs=qT[n * 64:(n + 1) * 64, h, qlo:T], start=True, stop=True),
                                    [("kT", kt)] + QT_ALL[qlo // 128:], [("ps", sb0 + n)])

                    emit_qk(0)
                    if nkt > 1:
                        emit_qk(1)
                    for g in range(nkt):
                        qlo, N, sb0 = info[g]
                        kt = g
                        pis = [(g % 3) * 2 + n for n in range(2)]
                        for n in range(2):
                            pi = pis[n]
                            S_.op("act", lambda n=n, pi=pi: nc.scalar.activation(
                                out=Pb[:, pi, 0:N], in_=banks[sb0 + n][:, 0:N], func=AF.Exp, scale=0.125),
                                [("ps", sb0 + n)], [("P", pi)])
                            if kt >= 4 * j:
                                S_.op("pool", lambda pi=pi: nc.gpsimd.memset(Pb[64:128, pi, 0:64], 0.0), [], [("P", pi)])
                        for _ in range(2 if g == 0 else 1):
                            if pending:
                                pending.pop(0)()
                        if g + 2 < nkt:
                            emit_qk(g + 2)
                        first, last = (kt == 0), (kt == nkt - 1)
                        PK = [("P", pis[0]), ("P", pis[1])]
                        for n in range(2):
                            pi = pis[n]
                            S_.op("pe", lambda n=n, pi=pi: nc.tensor.matmul(
                                banks[n][:, qlo:T], lhsT=Vs[:, kt, h * 128:(h + 1) * 128], rhs=Pb[:, pi, 0:N],
                                start=first, stop=last),
                                [("V", kt)] + PK, [("ps", n)])
                        p0, p1 = pis
                        S_.op("pe", lambda p1=p1: nc.tensor.matmul(
                            banks[B_L1][:, qlo:T], lhsT=ones[:], rhs=Pb[:, p1, 0:N], start=first, stop=last),
                            ["ones"] + PK, [("ps", B_L1)])
                        if g == 0:
                            S_.op("dve", lambda p0=p0: nc.vector.tensor_copy(Lacc[:, 0, :], Pb[:, p0, :]),
                                  [("P", p0)], [("Lacc", 0)])
                        else:
                            S_.op("dve", lambda p0=p0: nc.vector.tensor_tensor(
                                out=Lacc[:, 0, qlo:T], in0=Lacc[:, 0, qlo:T], in1=Pb[:, p0, 0:N], op=ALU.add),
                                [("P", p0), ("Lacc", 0)], [("Lacc", 0)])
                    while pending:
                        pending.pop(0)()
                    ep = make_epilogue(h)
                    ep[0]()
                    pending = ep[1:]
                for dc in range(8):
                    conv_out_part(j, s_co, [dc])
                    if pending:
                        pending.pop(0)()
                while pending:
                    pending.pop(0)()

            for j in range(NST):
                if j == 0:
                    for t in range(4):
                        norm_pre(xt[:, t, :], [("xt", t)], t)
                for t in range(4):
                    norm_T(t, C_GMIX, bk=(t if j > 0 else None))

                for qk, c0 in ((0, C_QN), (1, C_KN)):
                    for tt in range(4):
                        for half in range(2):
                            S_.op("dve", lambda qk=qk, c0=c0, tt=tt, half=half: nc.vector.tensor_tensor(
                                out=tabs[:, 2 * qk, tt, half * 32:(half + 1) * 32], in0=cos_t[:, j * 4 + tt, :],
                                in1=cst[:, c0 + half * 32:c0 + (half + 1) * 32], op=ALU.mult),
                                ["cos_t", "cst"], [("tab", 2 * qk, tt)])
                            S_.op("dve", lambda qk=qk, tt=tt, half=half: nc.vector.tensor_tensor(
                                out=tabs[:, 2 * qk + 1, tt, half * 32:(half + 1) * 32], in0=sin_t[:, j * 4 + tt, :],
                                in1=gsw[:, qk, half * 32:(half + 1) * 32], op=ALU.mult),
                                ["sin_t", ("gsw", qk, 0), ("gsw", qk, 1)], [("tab", 2 * qk + 1, tt)])

                phaseB(j)

                s_co = wload(10)
                if j > 0:
                    for t in range(4):
                        x_load(j, t)
                attention(j, s_co)

                s_ao = wload(11)
                wao = wring[:, s_ao, :].rearrange("p (k c) -> p k c", k=4)
                for dc in range(8):
                    b2 = new_bank()
                    for kc in range(4):
                        S_.op("pe", lambda kc=kc, dc=dc, b2=b2: nc.tensor.matmul(
                            banks[b2][:], lhsT=wao[:, kc, dc * 128:(dc + 1) * 128], rhs=onT[:, kc, :],
                            start=(kc == 0), stop=(kc == 3)), [("w", s_ao), ("onT", kc)], [("ps", b2)])
                    m2 = new_scr()
                    S_.op("dve", lambda dc=dc, b2=b2, m2=m2: nc.vector.tensor_tensor(
                        out=scr[:, m2, 0:T], in0=banks[b2][:], in1=big[:, 8 + dc, :], op=ALU.mult),
                        [("ps", b2), ("big", 8 + dc)], [("scr", m2)])
                    S_.op("dve", lambda dc=dc, m2=m2: nc.vector.tensor_tensor(
                        out=mixT[:, dc, :], in0=scr[:, m2, 0:T], in1=mixT[:, dc, :], op=ALU.add),
                        [("scr", m2), ("mixT", dc)], [("mixT", dc)])


                MIX_ALL = [("mixT", dc) for dc in range(8)]
                s_o = [wload(12), wload(13)]
                for t in range(4):
                    for c in range(2):
                        wv = wring[:, s_o[c], :].rearrange("p (k c) -> p k c", k=8)
                        bk = new_bank()
                        for kc in range(8):
                            S_.op("pe", lambda kc=kc, t=t, bk=bk, wv=wv: nc.tensor.matmul(
                                banks[bk][:], lhsT=mixT[:, kc, t * 128:(t + 1) * 128], rhs=wv[:, kc, :],
                                start=(kc == 0), stop=(kc == 7)), [("w", s_o[c])] + MIX_ALL, [("ps", bk)])
                        S_.op("dve", lambda t=t, bk=bk, c=c: nc.vector.tensor_tensor(
                            out=xt[:, t, c * 512:(c + 1) * 512], in0=banks[bk][:], in1=xt[:, t, c * 512:(c + 1) * 512],
                            op=ALU.add), [("ps", bk), ("xt", t)], [("xt", t)])
                    norm_pre(xt[:, t, :], [("xt", t)], t)
                    if t > 0:
                        norm_T(t - 1, C_GFFN)
                norm_T(3, C_GFFN)

                for jj in range(11):
                    s = wload(14 + jj)
                    wv = wring[:, s, :].rearrange("p (k c) -> p k c", k=8)
                    for c2 in range(2):
                        fi = jj * 2 + c2
                        bg = new_bank()
                        for kc in range(8):
                            S_.op("pe", lambda kc=kc, c2=c2, bg=bg, wv=wv: nc.tensor.matmul(
                                banks[bg][:], lhsT=wv[:, kc, c2 * 128:(c2 + 1) * 128], rhs=hT[:, kc, :],
                                start=(kc == 0), stop=(kc == 7)), [("w", s)] + HT_ALL, [("ps", bg)])
                        bu = new_bank()
                        for kc in range(8):
                            S_.op("pe", lambda kc=kc, c2=c2, bu=bu, wv=wv: nc.tensor.matmul(
                                banks[bu][:], lhsT=wv[:, kc, 256 + c2 * 128:256 + (c2 + 1) * 128], rhs=hT[:, kc, :],
                                start=(kc == 0), stop=(kc == 7)), [("w", s)] + HT_ALL, [("ps", bu)])
                        sg = new_scr()
                        S_.op("act", lambda bg=bg, sg=sg: nc.scalar.activation(out=scr[:, sg, 0:T], in_=banks[bg][:], func=AF.Silu),
                              [("ps", bg)], [("scr", sg)])
                        S_.op("dve", lambda bu=bu, sg=sg, fi=fi: nc.vector.tensor_tensor(
                            out=big[:, fi, :], in0=banks[bu][:], in1=scr[:, sg, 0:T], op=ALU.mult),
                            [("ps", bu), ("scr", sg)], [("big", fi)])

                for i in range(6):
                    s = wload(25 + i)
                    wv = wring[:, s, :].rearrange("p (k c) -> p k c", k=4)
                    for kl in range(4):
                        kc = 4 * i + kl
                        if kc >= NFC:
                            break
                        for t in range(4):
                            for c in range(2):
                                bk = t * 2 + c
                                S_.op("pe", lambda kc=kc, kl=kl, t=t, c=c, bk=bk, wv=wv: nc.tensor.matmul(
                                    banks[bk][:], lhsT=big[:, kc, t * 128:(t + 1) * 128], rhs=wv[:, kl, c * 512:(c + 1) * 512],
                                    start=(kc == 0), stop=(kc == NFC - 1)), [("w", s), ("big", kc)], [("ps", bk)])
                    if j + 1 < NST and 1 <= i <= 4:
                        stage_norm(j + 1, i - 1)
                for t in range(4):
                    for c in range(2):
                        bk = t * 2 + c
                        S_.op("dve", lambda t=t, c=c, bk=bk: nc.vector.tensor_tensor(
                            out=xt[:, t, c * 512:(c + 1) * 512], in0=banks[bk][:], in1=xt[:, t, c * 512:(c + 1) * 512],
                            op=ALU.add), [("ps", bk), ("xt", t)], [("xt", t)])
                    r0 = j * T + t * 128
                    S_.dma("act", [lambda t=t, r0=r0: nc.scalar.dma_start(out=out_d[r0:r0 + 128, :], in_=xt[:, t, :])],
                           [("xt", t)], [("out", j, t)], ("st", t))

        an = Sched(nc, es, needed=None)
        program(an)
        needed = an.analyze()
        em = Sched(nc, es, needed=needed, chans=list(an.chan_val.keys()))
        program(em)
        em.finish([("st", t) for t in range(4)])
    return nc


def _prep_weights(w_in, w_conv_out, w_attn_out, w_o, w_gate_up, w_down):
    blk = np.zeros((NBLK, 128, 4096), dtype=np.float32)

    def put(i, w2d):
        K, C = w2d.shape
        a = w2d.reshape(K // 128, 128, C).transpose(1, 0, 2).reshape(128, -1)
        blk[i, :, :a.shape[1]] = a

    for b in range(10):
        put(b, w_in[:, b * 512:(b + 1) * 512])
    put(10, w_conv_out)
    put(11, w_attn_out)
    for c in range(2):
        put(12 + c, w_o[:, c * 512:(c + 1) * 512])
    for jj in range(11):
        put(14 + jj, np.concatenate([w_gate_up[:, jj * 256:(jj + 1) * 256],
                                     w_gate_up[:, DFF + jj * 256:DFF + (jj + 1) * 256]], axis=1))
    for i in range(6):
        put(25 + i, w_down[i * 512:min((i + 1) * 512, DFF), :])
    return blk


def _prep_consts(g_mix, g_ffn, b_gate, conv_w, q_norm, k_norm, lq1, lk1, lq2, lk2, sub_norm):
    c = np.zeros((128, CW), dtype=np.float32)
    c[:, C_GMIX:C_GMIX + 8] = g_mix.reshape(8, 128).T
    c[:, C_GFFN:C_GFFN + 8] = g_ffn.reshape(8, 128).T
    c[:, C_BG:C_BG + 16] = b_gate.reshape(16, 128).T
    c[:, C_CW:C_CW + 12] = conv_w.reshape(3, 4, 128).transpose(2, 1, 0).reshape(128, 12)
    c[:, C_QN:C_QN + 64] = q_norm[None, :]
    c[:, C_KN:C_KN + 64] = k_norm[None, :]
    c[:, C_LAM:C_LAM + 256] = np.concatenate([lq1, lq2, lk1, lk2])[None, :]
    c[:, C_SN] = sub_norm
    inv = (1.0 / (np.float32(10000.0) ** (np.arange(0, 64, 2, dtype=np.float32) / np.float32(64)))).astype(np.float32)
    c[:, C_INV:C_INV + 32] = inv[None, :]
    return c


_NC_CACHE = {}


def kernel(x, g_mix, w_in, b_gate, conv_w, q_norm, k_norm, lambda_q1, lambda_k1, lambda_q2, lambda_k2,
           sub_norm, w_conv_out, w_attn_out, w_o, g_ffn, w_gate_up, w_down):
    f = lambda a: np.ascontiguousarray(np.asarray(a, dtype=np.float32))
    x = f(x)
    wblk = _prep_weights(f(w_in)[0], f(w_conv_out)[0], f(w_attn_out)[0], f(w_o)[0], f(w_gate_up)[0], f(w_down)[0])
    cst = _prep_consts(f(g_mix)[0], f(g_ffn)[0], f(b_gate)[0], f(conv_w)[0], f(q_norm)[0], f(k_norm)[0],
                       f(lambda_q1)[0], f(lambda_k1)[0], f(lambda_q2)[0], f(lambda_k2)[0], f(sub_norm)[0])
    if "nc" not in _NC_CACHE:
        _NC_CACHE["nc"] = build_program()
    nc = _NC_CACHE["nc"]
    in_maps = [{"x": x[c], "wblk": wblk, "cst": cst} for c in range(NCORES)]
    res = run_bass_kernel_spmd(nc, in_maps, core_ids=list(range(NCORES)))
    out = np.stack([np.asarray(res.results[c]["out"], dtype=np.float32) for c in range(NCORES)], axis=0)
    return out
```
